# Optimizing a Trainium2 kernel written in Bass

```python
import jax, jax.numpy as jnp
from jax import lax
import numpy as np

D_MODEL = 1024
BATCH = 4
SEQ = 8192
DEPTH = 2

CTX_LEN = 256
GRID_W = 64

HGRN_HEADS = 8
HGRN_DK = 128
HGRN_DV = 128
HGRN_WIDTH = HGRN_HEADS * HGRN_DK
HGRN_CHUNK = 64

LRU_WIDTH = 1024
LRU_BLOCKS = 8
LRU_BLOCK = LRU_WIDTH // LRU_BLOCKS
LRU_C = 8.0
CONV_W = 4
CONV_PAD = (2, 1)

N_EXPERTS = 16
EC_FACTOR = 2
EXPERT_FF = 1024

IN_COLS = 5 * HGRN_WIDTH + 2 * LRU_WIDTH + 2 * D_MODEL
DEEPNORM_ALPHA = (2.0 * DEPTH) ** 0.25
DEEPNORM_BETA = (8.0 * DEPTH) ** -0.25
LN_EPS = 1e-5
RMS_EPS = 1e-6

kernel_name = "hybrid_hgrn2_rglru_ecmoe_diffusion"


def layer_norm(x, g, b):
    xf = x.astype(jnp.float32)
    mu = jnp.mean(xf, axis=-1, keepdims=True)
    var = jnp.mean(jnp.square(xf - mu), axis=-1, keepdims=True)
    return ((xf - mu) * lax.rsqrt(var + LN_EPS) * g + b).astype(x.dtype)


def rms_norm(x, g):
    xf = x.astype(jnp.float32)
    return xf * lax.rsqrt(jnp.mean(jnp.square(xf), axis=-1, keepdims=True) + RMS_EPS) * g


def modulate(x, shift, scale):
    return x * (1.0 + scale) + shift


def flip(a):
    return jnp.flip(a, axis=1)


def heads(a):
    return a.reshape(a.shape[0], a.shape[1], HGRN_HEADS, -1)


def split_columns(p):
    widths = (HGRN_WIDTH,) * 5 + (LRU_WIDTH,) * 2 + (D_MODEL,) * 2
    return jnp.split(p, np.cumsum(widths)[:-1].tolist(), axis=-1)


def raster_to_colmajor(x):
    bsz, n, w = x.shape
    rows = n // GRID_W
    return x.reshape(bsz, rows, GRID_W, w).transpose(0, 2, 1, 3).reshape(bsz, n, w)


def colmajor_to_raster(x):
    bsz, n, w = x.shape
    rows = n // GRID_W
    return x.reshape(bsz, GRID_W, rows, w).transpose(0, 2, 1, 3).reshape(bsz, n, w)


def hgrn2_gates(f_pre, lb):
    f_pre = f_pre.astype(jnp.float32)
    log_f = jnp.logaddexp(jnp.log(lb), jnp.log1p(-lb) + jax.nn.log_sigmoid(f_pre))
    k = (1.0 - lb) * jax.nn.sigmoid(-f_pre)
    return log_f, k


def gla_chunk_scan(q, k, v, log_f, s0):
    bsz, t_len, h, _ = q.shape
    dv = v.shape[-1]
    nc = t_len // HGRN_CHUNK

    def to_chunks(a):
        return a.astype(jnp.float32).reshape(bsz, nc, HGRN_CHUNK, h, a.shape[-1]).transpose(1, 0, 3, 2, 4)

    lower_tri = jnp.tril(jnp.ones((HGRN_CHUNK, HGRN_CHUNK), bool))[:, :, None]

    def step(s, inp):
        qc, kc, vc, lf = inp
        b = jnp.cumsum(lf, axis=2)
        o_inter = jnp.einsum('bhtk,bhkv->bhtv', qc * jnp.exp(b), s)
        rel = jnp.where(lower_tri, b[:, :, :, None, :] - b[:, :, None, :, :], -jnp.inf)
        att = jnp.einsum('bhtk,bhtsk,bhsk->bhts', qc, jnp.exp(rel), kc)
        o = o_inter + jnp.einsum('bhts,bhsv->bhtv', att, vc)
        b_end = b[:, :, -1:, :]
        s_new = jnp.exp(b_end[:, :, 0, :, None]) * s + jnp.einsum('bhsk,bhsv->bhkv', kc * jnp.exp(b_end - b), vc)
        return s_new, o

    s_fin, o = lax.scan(step, s0.astype(jnp.float32), (to_chunks(q), to_chunks(k), to_chunks(v), to_chunks(log_f)))
    o = o.transpose(1, 0, 3, 2, 4).reshape(bsz, t_len, h, dv)
    return o, s_fin


def hgrn2_direction(q_c, v_c, fpre_c, q_l, v_l, fpre_l, lb, reverse):
    if reverse:
        q_c, v_c, fpre_c, q_l, v_l, fpre_l = (flip(a) for a in (q_c, v_c, fpre_c, q_l, v_l, fpre_l))
    s0 = jnp.zeros((q_c.shape[0], HGRN_HEADS, HGRN_DK, HGRN_DV), jnp.float32)
    lf, k = hgrn2_gates(fpre_c, lb)
    o_c, s_c = gla_chunk_scan(q_c, heads(k), v_c, heads(lf), s0)
    lf, k = hgrn2_gates(fpre_l, lb)
    o_l, _ = gla_chunk_scan(q_l, heads(k), v_l, heads(lf), s_c)
    if reverse:
        o_c, o_l = flip(o_c), flip(o_l)
    return o_c, o_l


def centred_dwconv(x, w, b):
    t_len = x.shape[1]
    xp = jnp.pad(x, ((0, 0), CONV_PAD, (0, 0)))
    return sum(xp[:, j:j + t_len] * w[j] for j in range(CONV_W)) + b


def linear_scan(a, u, h0):
    u = u.at[:, 0].add(a[:, 0] * h0)

    def combine(left, right):
        return left[0] * right[0], right[0] * left[1] + right[1]

    _, h = lax.associative_scan(combine, (a, u), axis=1)
    return h, h[:, -1]


def rg_lru(xc, w_a, b_a, w_x, b_x, lam, h0):
    bsz, t_len, w = xc.shape
    xf = xc.astype(jnp.float32)
    xb = xf.reshape(bsz, t_len, LRU_BLOCKS, LRU_BLOCK)
    r = jax.nn.sigmoid(jnp.einsum('btni,nij->btnj', xb, w_a).reshape(bsz, t_len, w) + b_a)
    i = jax.nn.sigmoid(jnp.einsum('btni,nij->btnj', xb, w_x).reshape(bsz, t_len, w) + b_x)
    log_a = -LRU_C * r * jax.nn.softplus(-lam)
    a = jnp.exp(log_a)
    u = jnp.sqrt(-jnp.expm1(2.0 * log_a)) * (i * xf)
    return linear_scan(a, u, h0)


def rglru_direction(x_c, x_l, w_a, b_a, w_x, b_x, lam, reverse):
    if reverse:
        x_c, x_l = flip(x_c), flip(x_l)
    h0 = jnp.zeros((x_c.shape[0], LRU_WIDTH), jnp.float32)
    h_c, hc_last = rg_lru(x_c, w_a, b_a, w_x, b_x, lam, h0)
    h_l, _ = rg_lru(x_l, w_a, b_a, w_x, b_x, lam, hc_last)
    if reverse:
        h_c, h_l = flip(h_c), flip(h_l)
    return h_c, h_l


def token_mixer(h_lat, h_ctx, w_in, lb, g_norm, conv_w, conv_b, lru_wa, lru_ba, lru_wx, lru_bx, lru_lam,
                w_branch_a, w_branch_b, w_out, need_ctx):
    q_l, v_l, ff_l, fb_l, og_l, lx_l, ly_l, ma_l, mb_l = split_columns(h_lat @ w_in)
    q_c, v_c, ff_c, fb_c, og_c, lx_c, ly_c, ma_c, mb_c = split_columns(h_ctx @ w_in)
    qh_l = heads(jax.nn.silu(q_l)) * HGRN_DK ** -0.5
    qh_c = heads(jax.nn.silu(q_c)) * HGRN_DK ** -0.5
    vh_l, vh_c = heads(v_l), heads(v_c)
    oc_f, ol_f = hgrn2_direction(qh_c, vh_c, ff_c, qh_l, vh_l, ff_l, lb[0], reverse=False)
    oc_b, ol_b = hgrn2_direction(qh_c, vh_c, fb_c, qh_l, vh_l, fb_l, lb[1], reverse=True)
    xc_l = centred_dwconv(raster_to_colmajor(lx_l), conv_w, conv_b)
    xc_c = centred_dwconv(lx_c, conv_w, conv_b)
    hc_f, hl_f = rglru_direction(xc_c, xc_l, lru_wa[0], lru_ba[0], lru_wx[0], lru_bx[0], lru_lam[0], reverse=False)
    hc_b, hl_b = rglru_direction(xc_c, xc_l, lru_wa[1], lru_ba[1], lru_wx[1], lru_bx[1], lru_lam[1], reverse=True)

    def merge(o_hgrn, og, h_lru, ly, ma, mb):
        bsz, t_len = og.shape[:2]
        o_a = rms_norm(o_hgrn, g_norm).reshape(bsz, t_len, HGRN_WIDTH) * jax.nn.silu(og.astype(jnp.float32))
        y_a = o_a.astype(og.dtype) @ w_branch_a
        y_b = (h_lru.astype(ly.dtype) * jax.nn.gelu(ly)) @ w_branch_b
        return (jax.nn.sigmoid(ma) * y_a + jax.nn.sigmoid(mb) * y_b) @ w_out

    y_lat = merge(ol_f + ol_b, og_l, colmajor_to_raster(hl_f + hl_b), ly_l, ma_l, mb_l)
    y_ctx = merge(oc_f + oc_b, og_c, hc_f + hc_b, ly_c, ma_c, mb_c) if need_ctx else None
    return y_lat, y_ctx


def expert_choice_ffn(h, w_router, w_gate, w_up, w_down):
    bsz, t_len, _ = h.shape
    cap = EC_FACTOR * t_len // N_EXPERTS
    affinity = jax.nn.softmax((h @ w_router).astype(jnp.float32), axis=-1)
    gate, idx = lax.top_k(affinity.transpose(0, 2, 1), cap)
    bidx = jnp.arange(bsz)[:, None, None]
    xs = h[bidx, idx]
    hid = jax.nn.silu(jnp.einsum('becd,edf->becf', xs, w_gate)) * jnp.einsum('becd,edf->becf', xs, w_up)
    out = jnp.einsum('becf,efd->becd', hid, w_down) * gate[..., None].astype(h.dtype)
    return jnp.zeros_like(h).at[bidx, idx].add(out)


def setup_inputs(seed: int = 0) -> dict:
    key = jax.random.key(seed)
    ks = jax.random.split(key, 26)
    nrm = jax.random.normal
    d, hw, lw, e, f = D_MODEL, HGRN_WIDTH, LRU_WIDTH, N_EXPERTS, EXPERT_FF
    u = jax.random.uniform(ks[15], (DEPTH, 2, lw), jnp.float32, 0.9, 0.999)
    a0 = u ** (1.0 / LRU_C)
    return {
        "x": nrm(ks[0], (BATCH, SEQ, d), jnp.float32),
        "c": nrm(ks[1], (BATCH, d), jnp.float32),
        "ctx": nrm(ks[2], (BATCH, CTX_LEN, d), jnp.float32),
        "c_ctx": nrm(ks[3], (d,), jnp.float32),
        "w_mod": nrm(ks[4], (DEPTH, d, 6 * d), jnp.float32) * (0.2 * d ** -0.5),
        "b_mod": nrm(ks[5], (DEPTH, 6 * d), jnp.float32) * 0.02,
        "w_in": nrm(ks[6], (DEPTH, d, IN_COLS), jnp.float32) * d ** -0.5,
        "hgrn_lb_logits": nrm(ks[7], (DEPTH, 2, hw), jnp.float32),
        "hgrn_norm_g": 1.0 + 0.02 * nrm(ks[8], (DEPTH, HGRN_DV), jnp.float32),
        "conv_w": nrm(ks[9], (DEPTH, CONV_W, lw), jnp.float32) * CONV_W ** -0.5,
        "conv_b": nrm(ks[10], (DEPTH, lw), jnp.float32) * 0.02,
        "lru_wa": nrm(ks[11], (DEPTH, 2, LRU_BLOCKS, LRU_BLOCK, LRU_BLOCK), jnp.float32) * LRU_BLOCK ** -0.5,
        "lru_ba": nrm(ks[12], (DEPTH, 2, lw), jnp.float32) * 0.02,
        "lru_wx": nrm(ks[13], (DEPTH, 2, LRU_BLOCKS, LRU_BLOCK, LRU_BLOCK), jnp.float32) * LRU_BLOCK ** -0.5,
        "lru_bx": nrm(ks[14], (DEPTH, 2, lw), jnp.float32) * 0.02,
        "lru_lambda": jnp.log(a0) - jnp.log1p(-a0),
        "w_branch_a": nrm(ks[16], (DEPTH, hw, d), jnp.float32) * (DEEPNORM_BETA * hw ** -0.5),
        "w_branch_b": nrm(ks[17], (DEPTH, lw, d), jnp.float32) * (DEEPNORM_BETA * lw ** -0.5),
        "w_out": nrm(ks[18], (DEPTH, d, d), jnp.float32) * (DEEPNORM_BETA * d ** -0.5),
        "ln_g": 1.0 + 0.02 * nrm(ks[19], (DEPTH, 2, d), jnp.float32),
        "ln_b": 0.02 * nrm(ks[20], (DEPTH, 2, d), jnp.float32),
        "w_router": nrm(ks[21], (DEPTH, d, e), jnp.float32) * d ** -0.5,
        "w_gate": nrm(ks[22], (DEPTH, e, d, f), jnp.float32) * d ** -0.5,
        "w_up": nrm(ks[23], (DEPTH, e, d, f), jnp.float32) * (DEEPNORM_BETA * d ** -0.5),
        "w_down": nrm(ks[24], (DEPTH, e, f, d), jnp.float32) * (DEEPNORM_BETA * f ** -0.5),
    }


def reference(x, c, ctx, c_ctx, w_mod, b_mod, w_in, hgrn_lb_logits, hgrn_norm_g, conv_w, conv_b,
              lru_wa, lru_ba, lru_wx, lru_bx, lru_lambda, w_branch_a, w_branch_b, w_out, ln_g, ln_b,
              w_router, w_gate, w_up, w_down):
    lb_cum = jnp.cumsum(jax.nn.softmax(hgrn_lb_logits.astype(jnp.float32), axis=0), axis=0)
    lower_bounds = lb_cum - lb_cum[0]
    for l in range(DEPTH):
        need_ctx = l < DEPTH - 1
        mod = jax.nn.silu(c) @ w_mod[l] + b_mod[l]
        mod_c = jax.nn.silu(c_ctx) @ w_mod[l] + b_mod[l]
        sh1, sc1, g1, sh2, sc2, g2 = jnp.split(mod[:, None, :], 6, axis=-1)
        csh1, csc1, cg1, csh2, csc2, cg2 = jnp.split(mod_c, 6)
        y_lat, y_ctx = token_mixer(
            modulate(x, sh1, sc1), modulate(ctx, csh1, csc1), w_in[l], lower_bounds[l], hgrn_norm_g[l],
            conv_w[l], conv_b[l], lru_wa[l], lru_ba[l], lru_wx[l], lru_bx[l], lru_lambda[l],
            w_branch_a[l], w_branch_b[l], w_out[l], need_ctx)
        x = layer_norm(DEEPNORM_ALPHA * x + g1 * y_lat, ln_g[l, 0], ln_b[l, 0])
        ffn_lat = expert_choice_ffn(modulate(x, sh2, sc2), w_router[l], w_gate[l], w_up[l], w_down[l])
        x = layer_norm(DEEPNORM_ALPHA * x + g2 * ffn_lat, ln_g[l, 1], ln_b[l, 1])
        if need_ctx:
            ctx = layer_norm(DEEPNORM_ALPHA * ctx + cg1 * y_ctx, ln_g[l, 0], ln_b[l, 0])
            ffn_ctx = expert_choice_ffn(modulate(ctx, csh2, csc2), w_router[l], w_gate[l], w_up[l], w_down[l])
            ctx = layer_norm(DEEPNORM_ALPHA * ctx + cg2 * ffn_ctx, ln_g[l, 1], ln_b[l, 1])
    return x
```

```python
import numpy as np
import concourse.bass as bass
import concourse.mybir as mybir
from concourse.bass_utils import run_bass_kernel_spmd
from contextlib import ExitStack

F32 = mybir.dt.float32
BF16 = mybir.dt.bfloat16
I32 = mybir.dt.int32
U32 = mybir.dt.uint32
AF = mybir.ActivationFunctionType
ALU = mybir.AluOpType
AX = mybir.AxisListType

SEM_CHUNK = 24000


class Buf:
    __slots__ = ("name", "w", "r", "dsem", "dcnt")

    def __init__(self, name):
        self.name = name
        self.w = None
        self.r = []
        self.dsem = None
        self.dcnt = 0


class Sched:
    ENG = ("pe", "act", "dve", "pool", "sp")

    def __init__(self, nc):
        self.nc = nc
        self.ops = {e: [] for e in self.ENG}
        self.dsems = []
        self.seen_c = {e: {} for e in self.ENG}
        self.seen_d = {e: {} for e in self.ENG}
        self.nbuf = 0
        self._dtot = {}
        self.NQ = {"sp": 40, "pool": 24, "act": 16}
        self.dma_base = {"sp": 0, "pool": 40, "act": 64}
        self.dma_n = {"sp": 0, "pool": 0, "act": 0}
        self.dsems = [None] * 80

    def buf(self, name=None):
        self.nbuf += 1
        return Buf(name or f"b{self.nbuf}")

    def _need(self, eng, tok, waits):
        if tok is None:
            return
        if tok[0] == "c":
            _, e, idx = tok
            if e == "pe" and eng == "pe":
                return
            if self.seen_c[eng].get(e, -1) >= idx:
                return
            if e == eng:
                pass
            self.seen_c[eng][e] = idx
            self.ops[e][idx]["signal"] = True
            waits.append(tok)
        else:
            _, slot, val = tok
            if self.seen_d[eng].get(slot, -1) >= val:
                return
            self.seen_d[eng][slot] = val
            waits.append(tok)

    def add(self, eng, fn, reads=(), writes=(), dma=False, waw=True):
        waits = []
        for b in reads:
            self._need(eng, b.w, waits)
        for b in writes:
            if waw:
                self._need(eng, b.w, waits)
            for t in b.r:
                self._need(eng, t, waits)
        idx = len(self.ops[eng])
        rec = {"fn": fn, "waits": waits, "signal": False, "dma": None}
        if dma:
            nq = self.NQ[eng]
            i = self.dma_n[eng]
            self.dma_n[eng] = i + 1
            slot = self.dma_base[eng] + (i % nq)
            val = 16 * (i // nq + 1)
            if val > 16:
                self._need(eng, ("d", slot, val - 16), waits)
            self._dtot[slot] = val
            tok = ("d", slot, val)
            rec["dma"] = slot
        else:
            tok = ("c", eng, idx)
        self.ops[eng].append(rec)
        for b in reads:
            b.r.append(tok)
            if len(b.r) > 12:
                b.r = self._compact(b.r)
        for b in writes:
            b.w = tok
            b.r = []
        return tok

    @staticmethod
    def _compact(toks):
        best = {}
        for t in toks:
            key = (t[0], t[1])
            if key not in best or best[key][2] < t[2]:
                best[key] = t
        return list(best.values())

    def barrier(self):
        last_c = {}
        for e in self.ENG:
            for i in range(len(self.ops[e]) - 1, -1, -1):
                if self.ops[e][i]["dma"] is None and self.ops[e][i]["fn"] is not None:
                    last_c[e] = i
                    break
        for e in self.ENG:
            waits = []
            for src, idx in last_c.items():
                self._need(e, ("c", src, idx), waits)
            for slot, val in self._dma_totals().items():
                self._need(e, ("d", slot, val), waits)
            if waits:
                self.ops[e].append({"fn": None, "waits": waits, "signal": False, "dma": None})

    def _dma_totals(self):
        return dict(self._dtot)

    def emit(self):
        nc = self.nc
        ndsem = len(self.dsems)
        dsem_h = [nc.alloc_semaphore(f"dq{i}") for i in range(ndsem)]
        csem = {}
        for e in self.ENG:
            n = 0
            for rec in self.ops[e]:
                if rec["dma"] is None and rec["signal"]:
                    rec["ord"] = n
                    n += 1
            nep = (n + SEM_CHUNK - 1) // SEM_CHUNK
            csem[e] = [nc.alloc_semaphore(f"c_{e}{k}") for k in range(max(nep, 1))]

        def run(e, engobj):
            for rec in self.ops[e]:
                for t in rec["waits"]:
                    if t[0] == "c":
                        o = self.ops[t[1]][t[2]]["ord"]
                        engobj.wait_ge(csem[t[1]][o // SEM_CHUNK], o % SEM_CHUNK + 1)
                    else:
                        engobj.wait_ge(dsem_h[t[1]], t[2])
                if rec["fn"] is None:
                    continue
                ins = rec["fn"](engobj)
                if rec["dma"] is not None:
                    ins.then_inc(dsem_h[rec["dma"]], 16)
                elif rec["signal"]:
                    o = rec["ord"]
                    ins.then_inc(csem[e][o // SEM_CHUNK], 1)

        with nc.Block() as block:
            @block.tensor
            def _(pe):
                run("pe", pe)

            @block.scalar
            def _(act):
                run("act", act)

            @block.vector
            def _(dve):
                run("dve", dve)

            @block.gpsimd
            def _(pool):
                run("pool", pool)

            @block.sync
            def _(sp):
                run("sp", sp)


D = 1024
NH = 8
CH = 64
GW = 64
NE = 16
LN_EPS = 1e-5
RMS_EPS = 1e-6
ALPHA = 4.0 ** 0.25
QSCALE = 128.0 ** -0.5
BIGIDX = 1.0e6


class Ring:
    def __init__(self, items):
        self.items = items
        self.i = 0

    def next(self):
        it = self.items[self.i % len(self.items)]
        self.i += 1
        return it


class K:
    pass


def build_program(T, TC, L=2, debug=False, nrun=None):
    nc = bass.Bass("TRN2", target_bir_lowering=False)
    k = K()
    k.nc, k.T, k.TC, k.L = nc, T, TC, L
    k.debug = bool(debug)
    TA = T + TC
    k.TA = TA
    k.ROWS = T // GW
    k.cap_l = 2 * T // NE
    k.cap_c = 2 * TC // NE
    k.SLOTS = k.cap_l + k.cap_c
    k.NST = (k.SLOTS + 127) // 128
    k.SLOTP = k.NST * 128
    S = Sched(nc)
    k.S = S

    def din(name, shape, dt=F32):
        return nc.dram_tensor(name, list(shape), dt, kind="ExternalInput").ap()

    def dscr(name, shape, dt=F32):
        if debug:
            return nc.dram_tensor(name, list(shape), dt, kind="ExternalOutput").ap()
        return nc.dram_tensor(name, list(shape), dt).ap()

    k.x = din("x", [T, D]); k.ctx = din("ctx", [TC, D])
    k.c = din("c", [D]); k.c_ctx = din("c_ctx", [D])
    k.w_mod = din("w_mod", [L, D, 6 * D]); k.b_mod = din("b_mod", [L, 6 * D])
    k.w_in = din("w_in", [L, D, 9 * D])
    k.lb_logits = din("hgrn_lb_logits", [L, 2, D]); k.norm_g = din("hgrn_norm_g", [L, 128])
    k.conv_w = din("conv_w", [L, 4, D]); k.conv_b = din("conv_b", [L, D])
    k.lru_wa = din("lru_wa", [L, 2, 8, 128, 128]); k.lru_ba = din("lru_ba", [L, 2, D])
    k.lru_wx = din("lru_wx", [L, 2, 8, 128, 128]); k.lru_bx = din("lru_bx", [L, 2, D])
    k.lru_lam = din("lru_lambda", [L, 2, D])
    k.w_ba = din("w_branch_a", [L, D, D]); k.w_bb = din("w_branch_b", [L, D, D]); k.w_out = din("w_out", [L, D, D])
    k.ln_g = din("ln_g", [L, 2, D]); k.ln_b = din("ln_b", [L, 2, D])
    k.w_router = din("w_router", [L, D, NE])
    k.w_gate = din("w_gate", [L, NE, D, D]); k.w_up = din("w_up", [L, NE, D, D]); k.w_down = din("w_down", [L, NE, D, D])
    k.out = nc.dram_tensor("out", [T, D], F32, kind="ExternalOutput").ap()
    k.b_in = S.buf("inputs")
    k.b_out = S.buf("out")

    def scr(name, shape, dt=F32):
        ap = dscr(name, shape, dt)
        return ap, S.buf(name)
    k.modv, k.b_modv = scr("modv", [L, 2, 6 * D])
    k.hT, k.b_hT = scr("hT", [D, TA], BF16)
    k.qs, k.b_qs = scr("qs", [D, TA], BF16)
    k.vtm, k.b_vtm = scr("vtm", [TA, D], BF16)
    k.lf = [None, None]; k.b_lf = [None, None]; k.kk = [None, None]; k.b_kk = [None, None]
    for d_ in range(2):
        k.lf[d_], k.b_lf[d_] = scr(f"lf{d_}", [D, TA])
        k.kk[d_], k.b_kk[d_] = scr(f"kk{d_}", [D, TA])
    k.sog, k.b_sog = scr("sog", [D, TA], BF16)
    k.lx, k.b_lx = scr("lx", [D, TA])
    k.gly, k.b_gly = scr("gly", [D, TA], BF16)
    k.sma, k.b_sma = scr("sma", [D, TA], BF16)
    k.smb, k.b_smb = scr("smb", [D, TA], BF16)
    k.od = [None, None]; k.b_od = [None, None]
    for d_ in range(2):
        k.od[d_], k.b_od[d_] = scr(f"od{d_}", [D, TA])
    k.gT, k.b_gT = scr("gT", [D, TA], BF16)
    k.x1, k.b_x1 = scr("x1", [TA, D])
    k.h2, k.b_h2 = scr("h2", [TA, D], BF16)
    k.xs, k.b_xs = scr("xs", [NE, k.SLOTP, D], BF16)
    k.ys, k.b_ys = scr("ys", [NE, k.SLOTP, D])
    k.xn, k.b_xn = scr("xn", [TA, D])
    k.affT, k.b_affT = scr("affT", [NE, TA])

    k.groups = [(0, TC)] + [(TC + i * 512, 512) for i in range(T // 512)]
    k.ntile = TA // 128

    with ExitStack() as es0:
        k.es0 = es0
        setup_consts(k)
        for l in range(L if nrun is None else nrun):
            last = (l == L - 1)
            pass_mod(k, l)
            pass_h(k, l)
            pass_inproj(k, l)
            pass_hgrn(k, l)
            pass_lru(k, l)
            pass_merge(k, l)
            pass_route(k, l)
            pass_experts(k, l)
            pass_combine(k, l)
        S.barrier()
        S.emit()
    return nc


_UID = [0]


def sbt(k, es, name, shape, dt):
    _UID[0] += 1
    name = f"{name}_{_UID[0]}"
    t = es.enter_context(k.nc.sbuf_tensor(name, list(shape), dt))
    return t, k.S.buf(name)


def pst(k, es, name, shape, dt=F32):
    _UID[0] += 1
    name = f"{name}_{_UID[0]}"
    t = es.enter_context(k.nc.psum_tensor(name, list(shape), dt))
    return t, k.S.buf(name)


STORE_Q = "act"
LRU_GN = 1024
HG_NI = 4
LRU_GN = 1024
SCALE_FUNC = AF.Copy


def dma(k, eng, out, in_, reads, writes, **kw):
    if eng == "act":
        eng = STORE_Q
    return k.S.add(eng, lambda e: e.dma_start(out=out, in_=in_, **kw), reads=reads, writes=writes, dma=True, waw=False)


def setup_consts(k):
    nc, S, es = k.nc, k.S, k.es0
    L = k.L
    k.identf, k.b_identf = sbt(k, es, "identf", [128, 128], F32)
    k.identb, k.b_identb = sbt(k, es, "identb", [128, 128], BF16)
    k.onesm, k.b_onesm = sbt(k, es, "onesm", [128, 128], F32)
    k.maskF, k.b_maskF = sbt(k, es, "maskF", [64, 8, 64], BF16)
    k.maskB, k.b_maskB = sbt(k, es, "maskB", [64, 8, 64], BF16)
    k.cmask, k.b_cmask = sbt(k, es, "cmask", [128, 8, 64], F32)
    with ExitStack() as es1:
        tmp, b_tmp = sbt(k, es1, "ctmp", [128, 8, 64], F32)
        S.add("pool", lambda e: e.memset(k.identf[:], 1.0), writes=[k.b_identf])
        S.add("pool", lambda e: e.affine_select(out=k.identf[:], in_=k.identf[:], pattern=[[-1, 128]], compare_op=ALU.is_equal, fill=0.0, base=0, channel_multiplier=1), reads=[k.b_identf], writes=[k.b_identf])
        S.add("dve", lambda e: e.tensor_copy(out=k.identb[:], in_=k.identf[:]), reads=[k.b_identf], writes=[k.b_identb])
        S.add("pool", lambda e: e.memset(k.onesm[:], 1.0 / 128.0), writes=[k.b_onesm])
        S.add("pool", lambda e: e.memset(tmp[:], 1.0), writes=[b_tmp])
        S.add("pool", lambda e: e.affine_select(out=tmp[0:64], in_=tmp[0:64], pattern=[[0, 8], [1, 64]], compare_op=ALU.is_ge, fill=0.0, base=0, channel_multiplier=-1), reads=[b_tmp], writes=[b_tmp])
        S.add("dve", lambda e: e.tensor_copy(out=k.maskF[:], in_=tmp[0:64]), reads=[b_tmp], writes=[k.b_maskF])
        S.add("pool", lambda e: e.memset(tmp[:], 1.0), writes=[b_tmp])
        S.add("pool", lambda e: e.affine_select(out=tmp[0:64], in_=tmp[0:64], pattern=[[0, 8], [-1, 64]], compare_op=ALU.is_ge, fill=0.0, base=0, channel_multiplier=1), reads=[b_tmp], writes=[b_tmp])
        S.add("dve", lambda e: e.tensor_copy(out=k.maskB[:], in_=tmp[0:64]), reads=[b_tmp], writes=[k.b_maskB])
        S.add("pool", lambda e: e.memset(k.cmask[:], 1.0), writes=[k.b_cmask])
        S.add("pool", lambda e: e.affine_select(out=k.cmask[:], in_=k.cmask[:], pattern=[[0, 8], [1, 64]], compare_op=ALU.is_gt, fill=0.0, base=0, channel_multiplier=0), reads=[k.b_cmask], writes=[k.b_cmask])
        S.barrier()
    def cols(name, src, n):
        t, b = sbt(k, es, name, [128, n], F32)
        dma(k, "sp", t[:], src.rearrange("n p -> p n"), [k.b_in], [b], allow_slow_non_contiguous=True)
        return t, b
    k.lbl, k.b_lbl = cols("lbl", k.lb_logits.rearrange("l d (h p) -> (l d h) p", p=128), L * 2 * 8)
    k.ng, k.b_ng = cols("ng", k.norm_g, L)
    k.cw, k.b_cw = cols("cw", k.conv_w.rearrange("l j (n p) -> (l j n) p", p=128), L * 4 * 8)
    k.cb, k.b_cb = cols("cb", k.conv_b.rearrange("l (n p) -> (l n) p", p=128), L * 8)
    k.ba, k.b_ba = cols("ba", k.lru_ba.rearrange("l d (n p) -> (l d n) p", p=128), L * 2 * 8)
    k.bx, k.b_bx = cols("bx", k.lru_bx.rearrange("l d (n p) -> (l d n) p", p=128), L * 2 * 8)
    k.lam, k.b_lam = cols("lam", k.lru_lam.rearrange("l d (n p) -> (l d n) p", p=128), L * 2 * 8)
    n = L * 2 * 8
    k.lb, k.b_lb = sbt(k, es, "lb", [128, n], F32)
    k.oml, k.b_oml = sbt(k, es, "oml", [128, n], F32)
    k.noml, k.b_noml = sbt(k, es, "noml", [128, n], F32)
    k.asc, k.b_asc = sbt(k, es, "asc", [128, n], F32)
    assert L == 2
    S.add("dve", lambda e: e.memset(k.lb[:], 0.0), writes=[k.b_lb])
    S.add("dve", lambda e: e.tensor_sub(out=k.lb[:, 16:32], in0=k.lbl[:, 16:32], in1=k.lbl[:, 0:16]), reads=[k.b_lbl, k.b_lb], writes=[k.b_lb])
    S.add("act", lambda e: e.activation(out=k.lb[:, 16:32], in_=k.lb[:, 16:32], func=AF.Sigmoid), reads=[k.b_lb], writes=[k.b_lb])
    S.add("dve", lambda e: e.tensor_scalar(out=k.oml[:], in0=k.lb[:], scalar1=-1.0, scalar2=1.0, op0=ALU.mult, op1=ALU.add), reads=[k.b_lb], writes=[k.b_oml])
    S.add("dve", lambda e: e.tensor_scalar(out=k.noml[:], in0=k.oml[:], scalar1=-1.0, scalar2=None, op0=ALU.mult), reads=[k.b_oml], writes=[k.b_noml])
    S.add("act", lambda e: e.activation(out=k.asc[:], in_=k.lam[:], func=AF.Exp, scale=-1.0), reads=[k.b_lam], writes=[k.b_asc])
    S.add("act", lambda e: e.activation(out=k.asc[:], in_=k.asc[:], func=AF.Ln, bias=1.0), reads=[k.b_asc], writes=[k.b_asc])
    S.add("dve", lambda e: e.tensor_scalar(out=k.asc[:], in0=k.asc[:], scalar1=-8.0, scalar2=None, op0=ALU.mult), reads=[k.b_asc], writes=[k.b_asc])
    k.idxi, k.b_idxi = sbt(k, es, "idxi", [128, k.ntile, NE], I32)
    k.gsel, k.b_gsel = sbt(k, es, "gsel", [128, k.ntile, NE], F32)
    k.afft, k.b_afft = sbt(k, es, "afft", [128, k.ntile, NE], F32)
    k.eoff, k.b_eoff = sbt(k, es, "eoff", [128, k.ntile, NE], F32)
    with ExitStack() as es2:
        eo_i, b_eo_i = sbt(k, es2, "eoff_i", [128, k.ntile, NE], I32)
        S.add("pool", lambda e: e.iota(eo_i[:], pattern=[[0, k.ntile], [k.SLOTP, NE]], base=0, channel_multiplier=0), writes=[b_eo_i])
        S.add("dve", lambda e: e.tensor_copy(out=k.eoff[:], in_=eo_i[:]), reads=[b_eo_i], writes=[k.b_eoff])
        S.barrier()
    k.epsln, k.b_epsln = sbt(k, es, "epsln", [128, 1], F32)
    k.epsrms, k.b_epsrms = sbt(k, es, "epsrms", [128, 1], F32)
    S.add("dve", lambda e: e.memset(k.epsln[:], LN_EPS), writes=[k.b_epsln])
    S.add("dve", lambda e: e.memset(k.epsrms[:], RMS_EPS), writes=[k.b_epsrms])


def tile_src(k, l, ti):
    if l == 0:
        t0 = ti * 128
        if t0 < k.TC:
            return k.ctx[t0:t0 + 128, :], k.b_in
        return k.x[t0 - k.TC:t0 - k.TC + 128, :], k.b_in
    return k.xn[ti * 128:(ti + 1) * 128, :], k.b_xn


def load_bc(k, es, l, names):
    idx = {"sh1": 0, "sc1": 1, "g1": 2, "sh2": 3, "sc2": 4, "g2": 5}
    res = {}
    for nm in names:
        for isctx in (0, 1):
            t, b = sbt(k, es, f"bc_{nm}{isctx}", [128, D], F32)
            src = k.modv[l, isctx:isctx + 1, idx[nm] * D:(idx[nm] + 1) * D].partition_broadcast(128)
            dma(k, "sp", t[:], src, [k.b_modv], [b])
            if nm in ("sc1", "sc2"):
                k.S.add("pool", lambda e, t=t: e.tensor_scalar(out=t[:], in0=t[:], scalar1=1.0, scalar2=None, op0=ALU.add), reads=[b], writes=[b])
            res[(nm, isctx)] = (t, b)
    return res


def load_row_bc(k, es, name, src_row):
    t, b = sbt(k, es, name, [128, D], F32)
    dma(k, "sp", t[:], src_row.partition_broadcast(128), [k.b_in], [b])
    return t, b


def pass_mod(k, l):
    nc, S = k.nc, k.S
    with ExitStack() as es:
        cc, b_cc = sbt(k, es, "m_cc", [128, 8, 2], F32)
        sc, b_sc = sbt(k, es, "m_sc", [128, 8, 2], F32)
        bm, b_bm = sbt(k, es, "m_bm", [2, 6 * D], F32)
        ms, b_ms = sbt(k, es, "m_ms", [2, 6 * D], F32)
        ws = [sbt(k, es, f"m_w{i}", [128, 8, 512], F32) for i in range(2)]
        pp = [pst(k, es, f"m_p{i}", [2, 512]) for i in range(2)]
        dma(k, "sp", cc[:, :, 0], k.c.rearrange("(k p) -> p k", p=128), [k.b_in], [b_cc], allow_slow_non_contiguous=True)
        dma(k, "sp", cc[:, :, 1], k.c_ctx.rearrange("(k p) -> p k", p=128), [k.b_in], [b_cc], allow_slow_non_contiguous=True)
        dma(k, "sp", bm[:], k.b_mod[l:l + 1, :].partition_broadcast(2), [k.b_in], [b_bm])
        S.add("act", lambda e: e.activation(out=sc[:], in_=cc[:], func=AF.Silu), reads=[b_cc], writes=[b_sc])
        for cg in range(12):
            w, b_w = ws[cg % 2]
            p, b_p = pp[cg % 2]
            dma(k, "sp", w[:], k.w_mod[l, :, cg * 512:(cg + 1) * 512].rearrange("(k p) c -> p k c", p=128), [k.b_in], [b_w])
            for kk in range(8):
                S.add("pe", lambda e, w=w, p=p, kk=kk: e.matmul(p[:], lhsT=sc[:, kk, :], rhs=w[:, kk, :], start=(kk == 0), stop=(kk == 7)),
                      reads=[b_sc, b_w], writes=[b_p])
            S.add("dve", lambda e, p=p, cg=cg: e.tensor_tensor(out=ms[:, cg * 512:(cg + 1) * 512], in0=p[:], in1=bm[:, cg * 512:(cg + 1) * 512], op=ALU.add),
                  reads=[b_p, b_bm], writes=[b_ms])
        dma(k, "act", k.modv[l], ms[:], [b_ms], [k.b_modv])
        S.barrier()


def pass_h(k, l):
    nc, S = k.nc, k.S
    with ExitStack() as es:
        bc = load_bc(k, es, l, ["sc1", "sh1"])
        xt = Ring([sbt(k, es, f"h_x{i}", [128, D], F32) for i in range(3)])
        tt = Ring([sbt(k, es, f"h_t{i}", [128, D], F32) for i in range(2)])
        hb = Ring([sbt(k, es, f"h_hb{i}", [128, D], BF16) for i in range(2)])
        ht = Ring([sbt(k, es, f"h_ht{i}", [128, 8, 128], BF16) for i in range(2)])
        pp = Ring([pst(k, es, f"h_p{i}", [128, 8, 128], BF16) for i in range(2)])
        hTv = k.hT.rearrange("(k p) t -> p k t", p=128)
        for ti in range(k.ntile):
            isctx = 1 if ti * 128 < k.TC else 0
            src, b_src = tile_src(k, l, ti)
            x_, b_x = xt.next(); t_, b_t = tt.next(); h_, b_h = hb.next(); o_, b_o = ht.next(); p_, b_p = pp.next()
            scp, b_scp = bc[("sc1", isctx)]; sh, b_sh = bc[("sh1", isctx)]
            dma(k, "sp", x_[:], src, [b_src], [b_x])
            S.add("dve", lambda e, t_=t_, x_=x_, scp=scp: e.tensor_tensor(out=t_[:], in0=x_[:], in1=scp[:], op=ALU.mult), reads=[b_x, b_scp], writes=[b_t])
            S.add("pool", lambda e, t_=t_, h_=h_, sh=sh: e.tensor_tensor(out=h_[:], in0=t_[:], in1=sh[:], op=ALU.add), reads=[b_t, b_sh], writes=[b_h])
            for kk in range(8):
                S.add("pe", lambda e, p_=p_, h_=h_, kk=kk: e.transpose(out=p_[:, kk, :], in_=h_[:, kk * 128:(kk + 1) * 128], identity=k.identb[:]),
                      reads=[b_h, k.b_identb], writes=[b_p])
            S.add("act", lambda e, o_=o_, p_=p_: e.activation(out=o_[:], in_=p_[:], func=AF.Copy), reads=[b_p], writes=[b_o])
            dma(k, "act", hTv[:, :, ti * 128:(ti + 1) * 128], o_[:], [b_o], [k.b_hT])
        S.barrier()


FAM = ["q", "v", "ff", "fb", "og", "lx", "ly", "ma", "mb"]


def pass_inproj(k, l):
    nc, S = k.nc, k.S
    with ExitStack() as es:
        wf = Ring([sbt(k, es, f"p_w{i}", [128, 8, D], BF16) for i in range(2)])
        hg = Ring([sbt(k, es, f"p_h{i}", [128, 8, 512], BF16) for i in range(2)])
        pp = Ring([pst(k, es, f"p_p{i}", [128, 512]) for i in range(4)])
        o32 = Ring([sbt(k, es, f"p_o32_{i}", [128, 8, 512], F32) for i in range(2)])
        o32b = Ring([sbt(k, es, f"p_o32b_{i}", [128, 8, 512], F32) for i in range(2)])
        sg = Ring([sbt(k, es, f"p_sg{i}", [128, 8, 512], F32) for i in range(1)])
        o16 = Ring([sbt(k, es, f"p_o16_{i}", [128, 8, 512], BF16) for i in range(2)])
        vt = Ring([sbt(k, es, f"p_vt{i}", [128, D], BF16) for i in range(2)])
        hTv = k.hT.rearrange("(k p) t -> p k t", p=128)
        win = k.w_in[l].rearrange("(k p) c -> p k c", p=128)

        def fm_view(ap, t0, n):
            return ap.rearrange("(c p) t -> p c t", p=128)[:, :, t0:t0 + n]

        for fi, fam in enumerate(FAM):
            w, b_w = wf.next()
            dma(k, "pool", w[:], win[:, :, fi * D:(fi + 1) * D], [k.b_in], [b_w])
            for (t0, n) in k.groups:
                h_, b_h = hg.next()
                dma(k, "sp", h_[:, :, 0:n], hTv[:, :, t0:t0 + n], [k.b_hT], [b_h])
                if fam == "v":
                    for tt in range(n // 128):
                        v_, b_v = vt.next()
                        for half in range(2):
                            p_, b_p = pp.next()
                            for kk in range(8):
                                S.add("pe", lambda e, p_=p_, h_=h_, w=w, kk=kk, tt=tt, half=half: e.matmul(
                                    p_[:, :], lhsT=h_[:, kk, tt * 128:(tt + 1) * 128], rhs=w[:, kk, half * 512:(half + 1) * 512],
                                    start=(kk == 0), stop=(kk == 7)), reads=[b_h, b_w], writes=[b_p])
                            S.add("act", lambda e, p_=p_, v_=v_, half=half: e.activation(out=v_[:, half * 512:(half + 1) * 512], in_=p_[:, :], func=AF.Copy),
                                  reads=[b_p], writes=[b_v])
                        dma(k, "act", k.vtm[t0 + tt * 128:t0 + (tt + 1) * 128, :], v_[:], [b_v], [k.b_vtm])
                    continue
                if fam in ("ff", "fb"):
                    d_ = 0 if fam == "ff" else 1
                    s_, b_s = sg.next(); lfo, b_lfo = o32.next(); ko, b_ko = o32b.next()
                    for cc in range(8):
                        p_, b_p = pp.next()
                        for kk in range(8):
                            S.add("pe", lambda e, p_=p_, h_=h_, w=w, kk=kk, cc=cc, n=n: e.matmul(
                                p_[:, 0:n], lhsT=w[:, kk, cc * 128:(cc + 1) * 128], rhs=h_[:, kk, 0:n], start=(kk == 0), stop=(kk == 7)),
                                reads=[b_h, b_w], writes=[b_p])
                        S.add("act", lambda e, p_=p_, s_=s_, cc=cc, n=n: e.activation(out=s_[:, cc, 0:n], in_=p_[:, 0:n], func=AF.Sigmoid), reads=[b_p], writes=[b_s])
                    for cc in range(8):
                        col = (l * 2 + d_) * 8 + cc
                        S.add("act", lambda e, s_=s_, lfo=lfo, cc=cc, n=n, col=col: e.activation(
                            out=lfo[:, cc, 0:n], in_=s_[:, cc, 0:n], func=AF.Ln, scale=k.oml[:, col:col + 1], bias=k.lb[:, col:col + 1]),
                            reads=[b_s, k.b_oml, k.b_lb], writes=[b_lfo])
                        S.add("dve", lambda e, s_=s_, ko=ko, cc=cc, n=n, col=col: e.tensor_scalar(
                            out=ko[:, cc, 0:n], in0=s_[:, cc, 0:n], scalar1=k.noml[:, col:col + 1], scalar2=k.oml[:, col:col + 1], op0=ALU.mult, op1=ALU.add),
                            reads=[b_s, k.b_oml, k.b_noml], writes=[b_ko])
                    dma(k, "act", fm_view(k.lf[d_], t0, n), lfo[:, :, 0:n], [b_lfo], [k.b_lf[d_]])
                    dma(k, "act", fm_view(k.kk[d_], t0, n), ko[:, :, 0:n], [b_ko], [k.b_kk[d_]])
                    continue
                func = {"q": AF.Silu, "og": AF.Silu, "lx": AF.Copy, "ly": AF.Gelu, "ma": AF.Sigmoid, "mb": AF.Sigmoid}[fam]
                if fam == "lx":
                    o_, b_o = o32.next(); dst, b_dst = k.lx, k.b_lx
                else:
                    o_, b_o = o16.next()
                    dst, b_dst = {"q": (k.qs, k.b_qs), "og": (k.sog, k.b_sog), "ly": (k.gly, k.b_gly), "ma": (k.sma, k.b_sma), "mb": (k.smb, k.b_smb)}[fam]
                for cc in range(8):
                    p_, b_p = pp.next()
                    for kk in range(8):
                        S.add("pe", lambda e, p_=p_, h_=h_, w=w, kk=kk, cc=cc, n=n: e.matmul(
                            p_[:, 0:n], lhsT=w[:, kk, cc * 128:(cc + 1) * 128], rhs=h_[:, kk, 0:n], start=(kk == 0), stop=(kk == 7)),
                            reads=[b_h, b_w], writes=[b_p])
                    S.add("act", lambda e, p_=p_, o_=o_, cc=cc, n=n, func=func: e.activation(out=o_[:, cc, 0:n], in_=p_[:, 0:n], func=func), reads=[b_p], writes=[b_o])
                dma(k, "act", fm_view(dst, t0, n), o_[:, :, 0:n], [b_o], [b_dst])
        S.barrier()


def pass_hgrn(k, l):
    nc, S = k.nc, k.S
    TC, T = k.TC, k.T
    with ExitStack() as es:
        lfr = Ring([sbt(k, es, f"g_lf{i}", [128, 512], F32) for i in range(2)])
        kr = Ring([sbt(k, es, f"g_k{i}", [128, 512], F32) for i in range(2)])
        qr = Ring([sbt(k, es, f"g_q{i}", [128, 512], BF16) for i in range(2)])
        br = Ring([sbt(k, es, f"g_b{i}", [128, 512], F32) for i in range(2)])
        cr = Ring([sbt(k, es, f"g_c{i}", [128, 512], F32) for i in range(2)])
        e1r = Ring([sbt(k, es, f"g_e1{i}", [128, 512], F32) for i in range(2)])
        e2r = Ring([sbt(k, es, f"g_e2{i}", [128, 512], F32) for i in range(2)])
        def hset(nm, shape, dt):
            return [[sbt(k, es, f"g_{nm}{p}_{h}", shape, dt) for h in range(NH)] for p in range(2)]
        qt = hset("qt", [128, 512], BF16); kt = hset("kt", [128, 512], BF16)
        ktm = hset("ktm", [64, 8, 128], BF16); vtm = hset("vtm", [64, 8, 128], BF16)
        attm = hset("att", [64, 512], BF16); eend = hset("ee", [128, 8], F32)
        s32 = [sbt(k, es, f"g_s32{h}", [128, 128], F32) for h in range(NH)]
        r32 = [sbt(k, es, f"g_r32{h}", [128, 128], F32) for h in range(NH)]
        sbf = [sbt(k, es, f"g_sbf{h}", [128, 128], BF16) for h in range(NH)]
        osb = Ring([sbt(k, es, f"g_o{i}", [128, 512], F32) for i in range(2)])
        pt = Ring([pst(k, es, f"g_pt{i}", [64, 8, 128], BF16) for i in range(1)])
        pa = Ring([pst(k, es, f"g_pa{i}", [64, 512]) for i in range(2)])
        po = [pst(k, es, f"g_po{i}", [128, 512]) for i in range(2)]
        psd = [pst(k, es, f"g_psd{i}", [128, 128]) for i in range(2)]
        cm = k.cmask[:].rearrange("p c t -> p (c t)")

        def prep_gen(d_, t0, n, par):
            nch = n // CH
            for h in range(NH):
                lf_, b_lf = lfr.next(); k_, b_k = kr.next(); q_, b_q = qr.next(); b_, b_b = br.next()
                e1, b_e1 = e1r.next(); e2, b_e2 = e2r.next()
                rows = slice(h * 128, (h + 1) * 128)
                dma(k, "sp", lf_[:, 0:n], k.lf[d_][rows, t0:t0 + n], [k.b_lf[d_]], [b_lf])
                dma(k, "sp", k_[:, 0:n], k.kk[d_][rows, t0:t0 + n], [k.b_kk[d_]], [b_k])
                dma(k, "sp", q_[:, 0:n], k.qs[rows, t0:t0 + n], [k.b_qs], [b_q])
                v_, b_v = vtm[par][h]
                dma(k, "sp", v_[:, 0:nch, :], k.vtm[t0:t0 + n, rows].rearrange("(c s) v -> s c v", s=CH), [k.b_vtm], [b_v])
                yield
                S.add("dve", lambda e, b_=b_, lf_=lf_, n=n: e.tensor_tensor_scan(
                    out=b_[:, 0:n], data0=cm[:, 0:n], data1=lf_[:, 0:n], initial=0.0, op0=ALU.mult, op1=ALU.add),
                    reads=[b_lf, k.b_cmask], writes=[b_b])
                yield
                ee, b_ee = eend[par][h]
                qt_, b_qt = qt[par][h]; kt_, b_kt = kt[par][h]
                if d_ == 0:
                    S.add("act", lambda e, e1=e1, b_=b_, n=n: e.activation(out=e1[:, 0:n], in_=b_[:, 0:n], func=AF.Exp), reads=[b_b], writes=[b_e1])
                    S.add("act", lambda e, e2=e2, b_=b_, n=n: e.activation(out=e2[:, 0:n], in_=b_[:, 0:n], func=AF.Exp, scale=-1.0), reads=[b_b], writes=[b_e2])
                    yield
                    S.add("act", lambda e, ee=ee, e1=e1, n=n, nch=nch: e.activation(
                        out=ee[:, 0:nch], in_=e1[:, 0:n].rearrange("p (c t) -> p c t", t=CH)[:, :, CH - 1], func=AF.Copy), reads=[b_e1], writes=[b_ee])
                else:
                    c_, b_c = cr.next()
                    S.add("dve", lambda e, c_=c_, b_=b_, lf_=lf_, n=n: e.tensor_sub(out=c_[:, 0:n], in0=b_[:, 0:n], in1=lf_[:, 0:n]), reads=[b_b, b_lf], writes=[b_c])
                    yield
                    S.add("act", lambda e, e1=e1, c_=c_, n=n: e.activation(out=e1[:, 0:n], in_=c_[:, 0:n], func=AF.Exp, scale=-1.0), reads=[b_c], writes=[b_e1])
                    S.add("act", lambda e, e2=e2, c_=c_, n=n: e.activation(out=e2[:, 0:n], in_=c_[:, 0:n], func=AF.Exp), reads=[b_c], writes=[b_e2])
                    S.add("act", lambda e, ee=ee, b_=b_, n=n, nch=nch: e.activation(
                        out=ee[:, 0:nch], in_=b_[:, 0:n].rearrange("p (c t) -> p c t", t=CH)[:, :, CH - 1], func=AF.Exp), reads=[b_b], writes=[b_ee])
                yield
                S.add("pool", lambda e, qt_=qt_, q_=q_, e1=e1, n=n: e.tensor_tensor(out=qt_[:, 0:n], in0=q_[:, 0:n], in1=e1[:, 0:n], op=ALU.mult), reads=[b_q, b_e1], writes=[b_qt])
                S.add("pool", lambda e, kt_=kt_, k_=k_, e2=e2, n=n: e.tensor_tensor(out=kt_[:, 0:n], in0=k_[:, 0:n], in1=e2[:, 0:n], op=ALU.mult), reads=[b_k, b_e2], writes=[b_kt])
                yield
                p_, b_p = pt.next()
                for ci in range(nch):
                    S.add("pe", lambda e, p_=p_, kt_=kt_, ci=ci: e.transpose(out=p_[:, ci, :], in_=kt_[:, ci * CH:(ci + 1) * CH], identity=k.identb[:]),
                          reads=[b_kt, k.b_identb], writes=[b_p])
                    if ci % 4 == 3:
                        yield
                km, b_km = ktm[par][h]
                S.add("act", lambda e, km=km, p_=p_, nch=nch: e.activation(out=km[:, 0:nch, :], in_=p_[:, 0:nch, :], func=AF.Copy), reads=[b_p], writes=[b_km])
                yield
                a_, b_a = pa.next()
                for ci in range(nch):
                    S.add("pe", lambda e, a_=a_, kt_=kt_, qt_=qt_, ci=ci: e.matmul(
                        a_[:, ci * CH:(ci + 1) * CH], lhsT=kt_[:, ci * CH:(ci + 1) * CH], rhs=qt_[:, ci * CH:(ci + 1) * CH], start=True, stop=True),
                        reads=[b_kt, b_qt], writes=[b_a])
                    if ci % 4 == 3:
                        yield
                am, b_am = attm[par][h]
                mk, b_mk = (k.maskF, k.b_maskF) if d_ == 0 else (k.maskB, k.b_maskB)
                mkv = mk[:].rearrange("s c t -> s (c t)")
                S.add("dve", lambda e, am=am, a_=a_, mkv=mkv, n=n: e.tensor_tensor(out=am[:, 0:n], in0=a_[:, 0:n], in1=mkv[:, 0:n], op=ALU.mult),
                      reads=[b_a, b_mk], writes=[b_am])
                yield

        def pump(gen, cnt):
            if gen is None:
                return None
            try:
                for _ in range(cnt):
                    next(gen)
            except StopIteration:
                return None
            return gen

        for d_ in range(2):
            for h in range(NH):
                S.add("pool", lambda e, h=h: e.memset(s32[h][0][:], 0.0), writes=[s32[h][1]])
                S.add("pool", lambda e, h=h: e.memset(sbf[h][0][:], 0.0), writes=[sbf[h][1]])
            if d_ == 0:
                order = list(k.groups)
            else:
                order = [k.groups[0]] + list(reversed(k.groups[1:]))
            g0 = prep_gen(d_, order[0][0], order[0][1], 0)
            while pump(g0, 1000) is not None:
                pass
            for gi, (t0, n) in enumerate(order):
                par = gi % 2
                nch = n // CH
                nxt = prep_gen(d_, order[gi + 1][0], order[gi + 1][1], 1 - par) if gi + 1 < len(order) else None
                for hp in range(NH // 2):
                    cis = range(nch) if d_ == 0 else range(nch - 1, -1, -1)
                    for ci in cis:
                        for j in range(2):
                            h = hp * 2 + j
                            po_, b_po = po[j]; sd_, b_sd = psd[j]
                            qt_, b_qt = qt[par][h]; km, b_km = ktm[par][h]; v_, b_v = vtm[par][h]; am, b_am = attm[par][h]
                            ee, b_ee = eend[par][h]; s_, b_s = s32[h]; r_, b_r = r32[h]; sb_, b_sb = sbf[h]
                            cs = slice(ci * CH, (ci + 1) * CH)
                            if d_ == 1:
                                S.add("pool", lambda e, r_=r_, s_=s_, ee=ee, ci=ci: e.tensor_scalar(out=r_[:], in0=s_[:], scalar1=ee[:, ci:ci + 1], scalar2=None, op0=ALU.mult),
                                      reads=[b_s, b_ee], writes=[b_r])
                                S.add("dve", lambda e, sb_=sb_, s_=s_, ee=ee, ci=ci: e.tensor_scalar(out=sb_[:], in0=s_[:], scalar1=ee[:, ci:ci + 1], scalar2=None, op0=ALU.mult),
                                      reads=[b_s, b_ee], writes=[b_sb])
                            S.add("pe", lambda e, po_=po_, v_=v_, am=am, ci=ci, cs=cs: e.matmul(po_[:, cs], lhsT=v_[:, ci, :], rhs=am[:, cs], start=True, stop=False),
                                  reads=[b_v, b_am], writes=[b_po])
                            S.add("pe", lambda e, po_=po_, sb_=sb_, qt_=qt_, cs=cs: e.matmul(po_[:, cs], lhsT=sb_[:], rhs=qt_[:, cs], start=False, stop=True),
                                  reads=[b_sb, b_qt], writes=[b_po])
                            S.add("pe", lambda e, sd_=sd_, km=km, v_=v_, ci=ci: e.matmul(sd_[:], lhsT=km[:, ci, :], rhs=v_[:, ci, :], start=True, stop=True),
                                  reads=[b_km, b_v], writes=[b_sd])
                            if d_ == 0:
                                S.add("dve", lambda e, r_=r_, s_=s_, sd_=sd_: e.tensor_tensor(out=r_[:], in0=sd_[:], in1=s_[:], op=ALU.add), reads=[b_sd, b_s], writes=[b_r])
                                S.add("dve", lambda e, sb_=sb_, r_=r_, ee=ee, ci=ci: e.tensor_scalar(out=sb_[:], in0=r_[:], scalar1=ee[:, ci:ci + 1], scalar2=None, op0=ALU.mult),
                                      reads=[b_r, b_ee], writes=[b_sb])
                                S.add("pool", lambda e, r_=r_, s_=s_, ee=ee, ci=ci: e.tensor_scalar(out=s_[:], in0=r_[:], scalar1=ee[:, ci:ci + 1], scalar2=None, op0=ALU.mult),
                                      reads=[b_r, b_ee], writes=[b_s])
                            else:
                                S.add("dve", lambda e, r_=r_, s_=s_, sd_=sd_: e.tensor_tensor(out=s_[:], in0=sd_[:], in1=r_[:], op=ALU.add), reads=[b_sd, b_r], writes=[b_s])
                            nxt = pump(nxt, 2)
                    for j in range(2):
                        h = hp * 2 + j
                        po_, b_po = po[j]
                        o_, b_o = osb.next()
                        S.add("act", lambda e, o_=o_, po_=po_, n=n: e.activation(out=o_[:, 0:n], in_=po_[:, 0:n], func=AF.Copy, scale=QSCALE), reads=[b_po], writes=[b_o])
                        dma(k, "act", k.od[d_][h * 128:(h + 1) * 128, t0:t0 + n], o_[:, 0:n], [b_o], [k.b_od[d_]])
                while nxt is not None:
                    nxt = pump(nxt, 1000)
        S.barrier()


def pass_lru(k, l):
    nc, S = k.nc, k.S
    TC, T, ROWS = k.TC, k.T, k.ROWS
    CO = 2
    LO = CO + TC + 4
    TOT = LO + T + 2
    GN = LRU_GN
    segs = [(CO, TC)] + [(LO + i * GN, GN) for i in range(T // GN)]
    with ExitStack() as es:
        big1, b_big1 = sbt(k, es, "l_big1", [128, TOT], F32)
        big2, b_big2 = sbt(k, es, "l_big2", [128, TOT], F32)
        xcb, b_xcb = sbt(k, es, "l_xcb", [128, TOT], BF16)
        hf, b_hf = sbt(k, es, "l_hf", [128, TOT], BF16)
        hb, b_hb = sbt(k, es, "l_hb", [128, TOT], BF16)
        gly, b_gly = sbt(k, es, "l_gly", [128, k.TA], BF16)
        tmpc, b_tmpc = sbt(k, es, "l_tmpc", [128, TC], F32)
        wts = [[sbt(k, es, f"l_w{d_}{g}", [128, 128], BF16) for g in range(2)] for d_ in range(2)]
        ii = Ring([sbt(k, es, f"l_ii{i}", [128, GN], F32) for i in range(2)])
        t1 = Ring([sbt(k, es, f"l_t1{i}", [128, GN], F32) for i in range(2)])
        ppr = Ring([pst(k, es, f"l_pr{i}", [128, GN]) for i in range(2)])
        ppi = Ring([pst(k, es, f"l_pi{i}", [128, GN]) for i in range(2)])
        for n in range(8):
            rows = slice(n * 128, (n + 1) * 128)
            dma(k, "sp", big1[:, 0:k.TA], k.lx[rows, :], [k.b_lx], [b_big1])
            dma(k, "sp", gly[:], k.gly[rows, :], [k.b_gly], [b_gly])
            for d_ in range(2):
                dma(k, "pool", wts[d_][0][0][:], k.lru_wa[l, d_, n], [k.b_in], [wts[d_][0][1]])
                dma(k, "pool", wts[d_][1][0][:], k.lru_wx[l, d_, n], [k.b_in], [wts[d_][1][1]])
            S.add("pool", lambda e: e.memset(big2[:], 0.0), writes=[b_big2])
            S.add("dve", lambda e: e.tensor_copy(out=big2[:, CO:CO + TC], in_=big1[:, 0:TC]), reads=[b_big1], writes=[b_big2])
            S.add("dve", lambda e: e.tensor_copy(out=big2[:, LO:LO + T].rearrange("p (c r) -> p c r", r=ROWS),
                                                  in_=big1[:, TC:TC + T].rearrange("p (r c) -> p c r", c=GW)), reads=[b_big1], writes=[b_big2])
            cws = [k.cw[:, (l * 4 + j) * 8 + n:(l * 4 + j) * 8 + n + 1] for j in range(4)]
            cwc = lambda j, cws=cws: cws[j]
            cbc = k.cb[:, l * 8 + n:l * 8 + n + 1]
            S.add("dve", lambda e, cwc=cwc, cbc=cbc: e.tensor_scalar(out=big1[:, 2:TOT - 1], in0=big2[:, 2:TOT - 1], scalar1=cwc(2), scalar2=cbc, op0=ALU.mult, op1=ALU.add),
                  reads=[b_big2, k.b_cw, k.b_cb], writes=[b_big1])
            for j, off in ((0, 0), (1, 1), (3, 3)):
                S.add("dve", lambda e, cwc=cwc, j=j, off=off: e.scalar_tensor_tensor(out=big1[:, 2:TOT - 1], in0=big2[:, off:TOT - 3 + off], scalar=cwc(j), in1=big1[:, 2:TOT - 1],
                                                                                  op0=ALU.mult, op1=ALU.add), reads=[b_big2, b_big1, k.b_cw], writes=[b_big1])
            S.add("pool", lambda e: e.tensor_copy(out=xcb[:, 2:TOT - 1], in_=big1[:, 2:TOT - 1]), reads=[b_big1], writes=[b_xcb])
            if n == 0 and l == 0:
                dump(k, "xc", big1[:, 0:TOT], b_big1, [128, TOT])
                dump(k, "xpad", big2[:, 0:TOT], b_big2, [128, TOT])
            for d_ in range(2):
                col = (l * 2 + d_) * 8 + n
                wa, b_wa = wts[d_][0]; wx, b_wx = wts[d_][1]
                for (s0, sn) in segs:
                    sl = slice(s0, s0 + sn)
                    pr, b_pr = ppr.next(); pi, b_pi = ppi.next()
                    i_, b_i = ii.next(); t_, b_t = t1.next()
                    for sub in range(0, sn, 512):
                        sw = min(512, sn - sub)
                        S.add("pe", lambda e, pr=pr, wa=wa, s0=s0, sub=sub, sw=sw: e.matmul(pr[:, sub:sub + sw], lhsT=wa[:], rhs=xcb[:, s0 + sub:s0 + sub + sw], start=True, stop=True),
                              reads=[b_wa, b_xcb], writes=[b_pr])
                        S.add("pe", lambda e, pi=pi, wx=wx, s0=s0, sub=sub, sw=sw: e.matmul(pi[:, sub:sub + sw], lhsT=wx[:], rhs=xcb[:, s0 + sub:s0 + sub + sw], start=True, stop=True),
                              reads=[b_wx, b_xcb], writes=[b_pi])
                    S.add("act", lambda e, pr=pr, sl=sl, sn=sn, col=col: e.activation(out=big1[:, sl], in_=pr[:, 0:sn], func=AF.Sigmoid, bias=k.ba[:, col:col + 1]), reads=[b_pr, k.b_ba], writes=[b_big1])
                    S.add("act", lambda e, i_=i_, pi=pi, sn=sn, col=col: e.activation(out=i_[:, 0:sn], in_=pi[:, 0:sn], func=AF.Sigmoid, bias=k.bx[:, col:col + 1]), reads=[b_pi, k.b_bx], writes=[b_i])
                    S.add("act", lambda e, sl=sl, col=col: e.activation(out=big1[:, sl], in_=big1[:, sl], func=AF.Exp, scale=k.asc[:, col:col + 1]), reads=[b_big1, k.b_asc], writes=[b_big1])
                    S.add("dve", lambda e, t_=t_, sl=sl, sn=sn: e.tensor_tensor(out=t_[:, 0:sn], in0=big1[:, sl], in1=big1[:, sl], op=ALU.mult), reads=[b_big1], writes=[b_t])
                    S.add("act", lambda e, t_=t_, sn=sn: e.activation(out=t_[:, 0:sn], in_=t_[:, 0:sn], func=AF.Sqrt, scale=-1.0, bias=1.0), reads=[b_t], writes=[b_t])
                    S.add("pool", lambda e, i_=i_, sl=sl, sn=sn: e.tensor_tensor(out=i_[:, 0:sn], in0=i_[:, 0:sn], in1=xcb[:, sl], op=ALU.mult), reads=[b_i, b_xcb], writes=[b_i])
                    S.add("dve", lambda e, t_=t_, i_=i_, sl=sl, sn=sn: e.tensor_tensor(out=big2[:, sl], in0=t_[:, 0:sn], in1=i_[:, 0:sn], op=ALU.mult), reads=[b_t, b_i], writes=[b_big2])
                ho, b_ho = (hf, b_hf) if d_ == 0 else (hb, b_hb)
                cseg = slice(CO, CO + TC); lseg = slice(LO, LO + T)
                if n == 0 and l == 0:
                    dump(k, f"a{d_}", big1[:, 0:TOT], b_big1, [128, TOT])
                    dump(k, f"u{d_}", big2[:, 0:TOT], b_big2, [128, TOT])
                if d_ == 0:
                    S.add("dve", lambda e: e.tensor_tensor_scan(out=tmpc[:], data0=big1[:, cseg], data1=big2[:, cseg], initial=0.0, op0=ALU.mult, op1=ALU.add),
                          reads=[b_big1, b_big2], writes=[b_tmpc])
                    S.add("dve", lambda e, ho=ho: e.tensor_tensor_scan(out=ho[:, lseg], data0=big1[:, lseg], data1=big2[:, lseg], initial=tmpc[:, TC - 1:TC], op0=ALU.mult, op1=ALU.add),
                          reads=[b_big1, b_big2, b_tmpc], writes=[b_ho])
                else:
                    S.add("dve", lambda e: e.tensor_tensor_scan(out=tmpc[:, ::-1], data0=big1[:, CO:CO + TC][:, ::-1], data1=big2[:, CO:CO + TC][:, ::-1], initial=0.0, op0=ALU.mult, op1=ALU.add),
                          reads=[b_big1, b_big2], writes=[b_tmpc])
                    S.add("dve", lambda e, ho=ho: e.tensor_tensor_scan(out=ho[:, LO:LO + T][:, ::-1], data0=big1[:, LO:LO + T][:, ::-1], data1=big2[:, LO:LO + T][:, ::-1],
                                                                       initial=tmpc[:, 0:1], op0=ALU.mult, op1=ALU.add),
                          reads=[b_big1, b_big2, b_tmpc], writes=[b_ho])
                S.add("pool", lambda e, ho=ho: e.tensor_copy(out=ho[:, cseg], in_=tmpc[:]), reads=[b_tmpc], writes=[b_ho])
            if n == 0 and l == 0:
                dump(k, "hf", hf[:, 0:TOT], b_hf, [128, TOT], BF16)
                dump(k, "hb", hb[:, 0:TOT], b_hb, [128, TOT], BF16)
            S.add("dve", lambda e: e.tensor_tensor(out=big1[:, CO:CO + TC], in0=hf[:, CO:CO + TC], in1=hb[:, CO:CO + TC], op=ALU.add), reads=[b_hf, b_hb], writes=[b_big1])
            S.add("dve", lambda e: e.tensor_tensor(out=big1[:, LO:LO + T], in0=hf[:, LO:LO + T], in1=hb[:, LO:LO + T], op=ALU.add), reads=[b_hf, b_hb], writes=[b_big1])
            S.add("dve", lambda e: e.tensor_tensor(out=xcb[:, 0:TC], in0=big1[:, CO:CO + TC], in1=gly[:, 0:TC], op=ALU.mult), reads=[b_big1, b_gly], writes=[b_xcb])
            S.add("dve", lambda e: e.tensor_tensor(out=xcb[:, TC:TC + T].rearrange("p (r c) -> p r c", c=GW),
                                                    in0=big1[:, LO:LO + T].rearrange("p (c r) -> p r c", r=ROWS),
                                                    in1=gly[:, TC:TC + T].rearrange("p (r c) -> p r c", c=GW), op=ALU.mult), reads=[b_big1, b_gly], writes=[b_xcb])
            dma(k, "act", k.gT[rows, :], xcb[:, 0:k.TA], [b_xcb], [k.b_gT])
        S.barrier()


def dump(k, name, ap, b, shape, dt=F32):
    if not k.debug:
        return
    d = k.nc.dram_tensor("dbg_" + name, list(shape), dt, kind="ExternalOutput").ap()
    dma(k, "sp", d, ap, [b], [k.S.buf("dbg_" + name)])


def bound_reg(k, eng, val):
    if not hasattr(k, "_breg"):
        k._breg = {}
    if val not in k._breg:
        k._breg[val] = eng.to_reg(val)
    return k._breg[val]


def emit_ln(k, z, b_z, xo, b_xo, lng, b_lng, lnb, b_lnb, st, b_st, mv, b_mv, rs, b_rs):
    S = k.S
    S.add("dve", lambda e: e.bn_stats(out=st[:, 0:6], in_=z[:, 0:512]), reads=[b_z], writes=[b_st])
    S.add("dve", lambda e: e.bn_stats(out=st[:, 6:12], in_=z[:, 512:1024]), reads=[b_z], writes=[b_st])
    S.add("dve", lambda e: e.bn_aggr(out=mv[:, 0:2], in_=st[:, 0:12]), reads=[b_st], writes=[b_mv])
    S.add("act", lambda e: e.activation(out=rs[:, 0:1], in_=mv[:, 1:2], func=AF.Sqrt, bias=k.epsln[:, 0:1]), reads=[b_mv, k.b_epsln], writes=[b_rs])
    S.add("dve", lambda e: e.reciprocal(out=rs[:, 1:2], in_=rs[:, 0:1]), reads=[b_rs], writes=[b_rs])
    S.add("dve", lambda e: e.tensor_scalar(out=xo[:], in0=z[:], scalar1=mv[:, 0:1], scalar2=rs[:, 1:2], op0=ALU.subtract, op1=ALU.mult), reads=[b_z, b_mv, b_rs], writes=[b_xo])
    S.add("pool", lambda e: e.tensor_tensor(out=xo[:], in0=xo[:], in1=lng[:], op=ALU.mult), reads=[b_xo, b_lng], writes=[b_xo])
    S.add("dve", lambda e: e.tensor_tensor(out=xo[:], in0=xo[:], in1=lnb[:], op=ALU.add), reads=[b_xo, b_lnb], writes=[b_xo])


def pass_merge(k, l):
    nc, S = k.nc, k.S
    last = (l == k.L - 1)
    with ExitStack() as es:
        bc = load_bc(k, es, l, ["g1", "sc2", "sh2"])
        lng, b_lng = load_row_bc(k, es, "r_lng", k.ln_g[l, 0:1, :])
        lnb, b_lnb = load_row_bc(k, es, "r_lnb", k.ln_b[l, 0:1, :])
        wa, b_wa = sbt(k, es, "r_wa", [128, 8, D], BF16)
        wb, b_wb = sbt(k, es, "r_wb", [128, 8, D], BF16)
        wo, b_wo = sbt(k, es, "r_wo", [128, 8, D], BF16)
        wr, b_wr = sbt(k, es, "r_wr", [128, 8, NE], F32)
        dma(k, "pool", wa[:], k.w_ba[l].rearrange("(k p) c -> p k c", p=128), [k.b_in], [b_wa])
        dma(k, "pool", wb[:], k.w_bb[l].rearrange("(k p) c -> p k c", p=128), [k.b_in], [b_wb])
        dma(k, "pool", wo[:], k.w_out[l].rearrange("(k p) c -> p k c", p=128), [k.b_in], [b_wo])
        dma(k, "sp", wr[:], k.w_router[l].rearrange("(k p) c -> p k c", p=128), [k.b_in], [b_wr])
        N = 128
        sets = []
        for i in range(2):
            o0, b_o0 = sbt(k, es, f"r_o0{i}", [128, 8, N], F32)
            o1, b_o1 = sbt(k, es, f"r_o1{i}", [128, 8, N], F32)
            sog, b_sog = sbt(k, es, f"r_sog{i}", [128, 8, N], BF16)
            gt, b_gt = sbt(k, es, f"r_gt{i}", [128, 8, N], BF16)
            sma, b_sma = sbt(k, es, f"r_sma{i}", [128, 8, N], BF16)
            smb, b_smb = sbt(k, es, f"r_smb{i}", [128, 8, N], BF16)
            oa, b_oa = sbt(k, es, f"r_oa{i}", [128, 8, N], BF16)
            mT, b_mT = sbt(k, es, f"r_mT{i}", [128, 8, N], BF16)
            sets.append((o0, b_o0, o1, b_o1, sog, b_sog, gt, b_gt, sma, b_sma, smb, b_smb, oa, b_oa, mT, b_mT))
        gset = Ring(sets)
        rsq = Ring([sbt(k, es, f"r_rsq{i}", [128, N], F32) for i in range(2)])
        rst = Ring([sbt(k, es, f"r_rst{i}", [128, N], F32) for i in range(2)])
        oa1 = Ring([sbt(k, es, f"r_oa1{i}", [128, N], F32) for i in range(2)])
        ma_ = Ring([sbt(k, es, f"r_ma{i}", [128, N], F32) for i in range(2)])
        mb_ = Ring([sbt(k, es, f"r_mb{i}", [128, N], F32) for i in range(2)])
        xt = Ring([sbt(k, es, f"r_x{i}", [128, D], F32) for i in range(2)])
        yv = Ring([sbt(k, es, f"r_yv{i}", [128, D], F32) for i in range(1)])
        zt = Ring([sbt(k, es, f"r_z{i}", [128, D], F32) for i in range(1)])
        x1t = Ring([sbt(k, es, f"r_x1{i}", [128, D], F32) for i in range(2)])
        h2f = Ring([sbt(k, es, f"r_h2f{i}", [128, D], F32) for i in range(2)])
        h2b = Ring([sbt(k, es, f"r_h2b{i}", [128, D], BF16) for i in range(2)])
        h2T = Ring([sbt(k, es, f"r_h2T{i}", [128, 8, 128], F32) for i in range(2)])
        stt = Ring([sbt(k, es, f"r_st{i}", [128, 12], F32) for i in range(2)])
        mvt = Ring([sbt(k, es, f"r_mv{i}", [128, 2], F32) for i in range(2)])
        rss = Ring([sbt(k, es, f"r_rs{i}", [128, 2], F32) for i in range(2)])
        sm = Ring([sbt(k, es, f"r_sm{i}", [128, 4], F32) for i in range(2)])
        ex = Ring([sbt(k, es, f"r_ex{i}", [128, NE], F32) for i in range(2)])
        aTs = Ring([sbt(k, es, f"r_aTs{i}", [NE, 128], F32) for i in range(2)])
        pm, b_pm = pst(k, es, "r_pm", [128, 512])
        pya, b_pya = pst(k, es, "r_pya", [128, 512])
        pyb, b_pyb = pst(k, es, "r_pyb", [128, 512])
        py = Ring([pst(k, es, f"r_py{i}", [128, 512]) for i in range(2)])
        ptr = [pst(k, es, f"r_ptr{i}", [128, 4, 128]) for i in range(2)]
        psm, b_psm = pst(k, es, "r_psm", [128, 512])

        def fmv(ap, t0, n):
            return ap.rearrange("(c p) t -> p c t", p=128)[:, :, t0:t0 + n]

        pendingB = [None]

        def group_body(gi, o0, b_o0, o1, b_o1, sog, b_sog, gt, b_gt, sma, b_sma, smb, b_smb, oa, b_oa, mT, b_mT):
            t0 = gi * N
            isctx = 1 if t0 < k.TC else 0
            if isctx and last:
                return None
            def f1():
                dma(k, "sp", o0[:], fmv(k.od[0], t0, N), [k.b_od[0]], [b_o0])
                dma(k, "sp", o1[:], fmv(k.od[1], t0, N), [k.b_od[1]], [b_o1])
                dma(k, "sp", sog[:], fmv(k.sog, t0, N), [k.b_sog], [b_sog])
                dma(k, "sp", gt[:], fmv(k.gT, t0, N), [k.b_gT], [b_gt])
                dma(k, "sp", sma[:], fmv(k.sma, t0, N), [k.b_sma], [b_sma])
                dma(k, "sp", smb[:], fmv(k.smb, t0, N), [k.b_smb], [b_smb])
                S.add("pool", lambda e: e.tensor_tensor(out=o0[:], in0=o0[:], in1=o1[:], op=ALU.add), reads=[b_o0, b_o1], writes=[b_o0])
                S.add("act", lambda e: e.activation(out=o1[:], in_=o0[:], func=AF.Square), reads=[b_o0], writes=[b_o1])
                for h in range(NH):
                    q_, b_q = rsq.next(); r_, b_r = rst.next(); a1, b_a1 = oa1.next()
                    S.add("pe", lambda e, h=h: e.matmul(pm[:, 0:N], lhsT=k.onesm[:], rhs=o1[:, h, :], start=True, stop=True), reads=[k.b_onesm, b_o1], writes=[b_pm])
                    S.add("act", lambda e, q_=q_: e.activation(out=q_[:], in_=pm[:, 0:N], func=AF.Sqrt, bias=k.epsrms[:, 0:1]), reads=[b_pm, k.b_epsrms], writes=[b_q])
                    S.add("dve", lambda e, q_=q_, r_=r_: e.reciprocal(out=r_[:], in_=q_[:]), reads=[b_q], writes=[b_r])
                    S.add("dve", lambda e, a1=a1, r_=r_, h=h: e.tensor_tensor(out=a1[:], in0=o0[:, h, :], in1=r_[:], op=ALU.mult), reads=[b_o0, b_r], writes=[b_a1])
                    S.add("dve", lambda e, a1=a1, h=h: e.scalar_tensor_tensor(out=oa[:, h, :], in0=a1[:], scalar=k.ng[:, l:l + 1], in1=sog[:, h, :], op0=ALU.mult, op1=ALU.mult),
                          reads=[b_a1, k.b_ng, b_sog], writes=[b_oa])
                    yield
                yield
            hold = {}

            def f2():
                for dc in range(8):
                    for c in range(8):
                        S.add("pe", lambda e, dc=dc, c=c: e.matmul(pya[:, 0:N], lhsT=wa[:, c, dc * 128:(dc + 1) * 128], rhs=oa[:, c, :], start=(c == 0), stop=(c == 7)),
                              reads=[b_wa, b_oa], writes=[b_pya])
                    for c in range(8):
                        S.add("pe", lambda e, dc=dc, c=c: e.matmul(pyb[:, 0:N], lhsT=wb[:, c, dc * 128:(dc + 1) * 128], rhs=gt[:, c, :], start=(c == 0), stop=(c == 7)),
                              reads=[b_wb, b_gt], writes=[b_pyb])
                    m1, b_m1 = ma_.next(); m2, b_m2 = mb_.next()
                    S.add("dve", lambda e, m1=m1, dc=dc: e.tensor_tensor(out=m1[:], in0=pya[:, 0:N], in1=sma[:, dc, :], op=ALU.mult), reads=[b_pya, b_sma], writes=[b_m1])
                    S.add("dve", lambda e, m2=m2, dc=dc: e.tensor_tensor(out=m2[:], in0=pyb[:, 0:N], in1=smb[:, dc, :], op=ALU.mult), reads=[b_pyb, b_smb], writes=[b_m2])
                    S.add("pool", lambda e, m1=m1, m2=m2, dc=dc: e.tensor_tensor(out=mT[:, dc, :], in0=m1[:], in1=m2[:], op=ALU.add), reads=[b_m1, b_m2], writes=[b_mT])
                    yield

            def f3():
                for tt in range(N // 128):
                    ti = (t0 + tt * 128) // 128
                    g1, b_g1 = bc[("g1", isctx)]; scp, b_scp = bc[("sc2", isctx)]; sh, b_sh = bc[("sh2", isctx)]
                    y_, b_y = yv.next(); x_, b_x = xt.next(); z_, b_z = zt.next(); x1_, b_x1 = x1t.next()
                    for half in range(2):
                        p_, b_p = py.next()
                        for c in range(8):
                            S.add("pe", lambda e, p_=p_, c=c, tt=tt, half=half: e.matmul(p_[:, :], lhsT=mT[:, c, tt * 128:(tt + 1) * 128], rhs=wo[:, c, half * 512:(half + 1) * 512],
                                                                                       start=(c == 0), stop=(c == 7)), reads=[b_mT, b_wo], writes=[b_p])
                        S.add("dve", lambda e, p_=p_, y_=y_, g1=g1, half=half: e.tensor_tensor(out=y_[:, half * 512:(half + 1) * 512], in0=p_[:, :], in1=g1[:, half * 512:(half + 1) * 512], op=ALU.mult),
                              reads=[b_p, b_g1], writes=[b_y])
                        yield
                    src, b_src = tile_src(k, l, ti)
                    dma(k, "sp", x_[:], src, [b_src], [b_x])
                    S.add("dve", lambda e, z_=z_, x_=x_, y_=y_: e.scalar_tensor_tensor(out=z_[:], in0=x_[:], scalar=ALPHA, in1=y_[:], op0=ALU.mult, op1=ALU.add), reads=[b_x, b_y], writes=[b_z])
                    st, b_st = stt.next(); mv, b_mv = mvt.next(); rs, b_rs = rss.next()
                    emit_ln(k, z_, b_z, x1_, b_x1, lng, b_lng, lnb, b_lnb, st, b_st, mv, b_mv, rs, b_rs)
                    dma(k, "act", k.x1[ti * 128:(ti + 1) * 128, :], x1_[:], [b_x1], [k.b_x1])
                    yield
                    hf_, b_hf = h2f.next(); hb_, b_hb = h2b.next()
                    S.add("pool", lambda e, hf_=hf_, x1_=x1_, scp=scp: e.tensor_tensor(out=hf_[:], in0=x1_[:], in1=scp[:], op=ALU.mult), reads=[b_x1, b_scp], writes=[b_hf])
                    S.add("dve", lambda e, hf_=hf_, sh=sh: e.tensor_tensor(out=hf_[:], in0=hf_[:], in1=sh[:], op=ALU.add), reads=[b_hf, b_sh], writes=[b_hf])
                    S.add("act", lambda e, hf_=hf_, hb_=hb_: e.activation(out=hb_[:], in_=hf_[:], func=AF.Copy), reads=[b_hf], writes=[b_hb])
                    dma(k, "act", k.h2[ti * 128:(ti + 1) * 128, :], hb_[:], [b_hb], [k.b_h2])
                    yield
                    def stageB(ti=ti, hf_=hf_, b_hf=b_hf):
                        hT_, b_hT = h2T.next()
                        for kk in range(8):
                            pt_, b_pt = ptr[kk // 4]
                            S.add("pe", lambda e, pt_=pt_, hf_=hf_, kk=kk: e.transpose(out=pt_[:, kk % 4, :], in_=hf_[:, kk * 128:(kk + 1) * 128], identity=k.identf[:]),
                                  reads=[b_hf, k.b_identf], writes=[b_pt])
                        for j in range(2):
                            pt_, b_pt = ptr[j]
                            S.add("act", lambda e, pt_=pt_, hT_=hT_, j=j: e.activation(out=hT_[:, j * 4:(j + 1) * 4, :], in_=pt_[:], func=AF.Copy), reads=[b_pt], writes=[b_hT])
                        yield
                        for kk in range(8):
                            S.add("pe", lambda e, hT_=hT_, kk=kk: e.matmul(psm[:, 0:NE], lhsT=hT_[:, kk, :], rhs=wr[:, kk, :], start=(kk == 0), stop=(kk == 7)), reads=[b_hT, b_wr], writes=[b_psm])
                        yield
                        s_, b_s = sm.next(); ex_, b_ex = ex.next()
                        S.add("dve", lambda e, s_=s_: e.tensor_reduce(out=s_[:, 0:1], in_=psm[:, 0:NE], axis=AX.X, op=ALU.max, negate=True), reads=[b_psm], writes=[b_s])
                        S.add("act", lambda e, s_=s_, ex_=ex_: e.activation(out=ex_[:], in_=psm[:, 0:NE], func=AF.Exp, bias=s_[:, 0:1], accum_out=s_[:, 1:2]), reads=[b_psm, b_s], writes=[b_ex, b_s])
                        S.add("dve", lambda e, s_=s_: e.reciprocal(out=s_[:, 2:3], in_=s_[:, 1:2]), reads=[b_s], writes=[b_s])
                        S.add("dve", lambda e, s_=s_, ex_=ex_, ti=ti: e.tensor_scalar(out=k.afft[:, ti, :], in0=ex_[:], scalar1=s_[:, 2:3], scalar2=None, op0=ALU.mult), reads=[b_ex, b_s], writes=[k.b_afft])
                        S.add("pe", lambda e, ti=ti: e.transpose(out=psm[0:NE, 128:256], in_=k.afft[:, ti, :], identity=k.identf[:]), reads=[k.b_afft, k.b_identf], writes=[b_psm])
                        at_, b_at = aTs.next()
                        S.add("act", lambda e, at_=at_: e.activation(out=at_[:], in_=psm[0:NE, 128:256], func=AF.Copy), reads=[b_psm], writes=[b_at])
                        dma(k, "act", k.affT[:, ti * 128:(ti + 1) * 128], at_[:], [b_at], [k.b_affT])
                        yield
                    hold['B'] = stageB

            return [f1, f2, f3, lambda: hold['B']()]
        tiles_st = []
        def run_round():
            gens = []
            for ent in list(tiles_st):
                gens.append(ent.pop(0)())
                if not ent:
                    tiles_st.remove(ent)
            while gens:
                for g in list(gens):
                    try:
                        next(g)
                    except StopIteration:
                        gens.remove(g)
        for gi in range(k.TA // N):
            st = group_body(gi, *gset.next())
            if st is not None:
                tiles_st.append(st)
            run_round()
        while tiles_st:
            run_round()
        S.barrier()


def pass_route(k, l):
    nc, S = k.nc, k.S
    last = (l == k.L - 1)
    TC, T, TA = k.TC, k.T, k.TA
    with ExitStack() as es:
        aff, b_aff = sbt(k, es, "t_aff", [NE, TA], F32)
        cmp_, b_cmp = sbt(k, es, "t_cmp", [NE, T], F32)
        sel, b_sel = sbt(k, es, "t_sel", [NE, T], F32)
        pos, b_pos = sbt(k, es, "t_pos", [NE, T], F32)
        idxf, b_idxf = sbt(k, es, "t_idxf", [NE, TA], F32)
        sv, b_sv = sbt(k, es, "t_sv", [NE, 8], F32)
        idxt, b_idxt = sbt(k, es, "t_idxt", [128, k.ntile, NE], F32)
        h2t = Ring([sbt(k, es, f"t_h2{i}", [128, D], BF16) for i in range(3)])
        pidx, b_pidx = pst(k, es, "t_pidx", [128, 32, NE])
        dma(k, "sp", aff[:], k.affT[:, :], [k.b_affT], [b_aff])
        sets = [(TC, T, k.cap_l, 0)]
        if not last:
            sets.append((0, TC, k.cap_c, k.cap_l))
        for (c0, n, cap, soff) in sets:
            av = aff[:, c0:c0 + n]
            lo, hi, mid, cnt, ge, dd, ee, ss = [sv[:, i:i + 1] for i in range(8)]
            S.add("dve", lambda e, lo=lo: e.memset(lo, 0.0), writes=[b_sv])
            S.add("dve", lambda e, hi=hi: e.memset(hi, 1.0), writes=[b_sv])
            S.add("dve", lambda e, mid=mid: e.memset(mid, 0.5), writes=[b_sv])
            for it in range(34):
                S.add("dve", lambda e, av=av, n=n, mid=mid, cnt=cnt: e.tensor_scalar(out=cmp_[:, 0:n], in0=av, scalar1=mid, scalar2=None, op0=ALU.is_ge, op1=ALU.add, accum_out=cnt),
                      reads=[b_aff, b_sv], writes=[b_cmp, b_sv])
                S.add("dve", lambda e, ge=ge, cnt=cnt, cap=cap: e.tensor_scalar(out=ge, in0=cnt, scalar1=float(cap) - 0.5, scalar2=None, op0=ALU.is_ge), reads=[b_sv], writes=[b_sv])
                S.add("dve", lambda e, dd=dd, mid=mid, lo=lo: e.tensor_sub(out=dd, in0=mid, in1=lo), reads=[b_sv], writes=[b_sv])
                S.add("dve", lambda e, ee=ee, mid=mid, hi=hi: e.tensor_sub(out=ee, in0=hi, in1=mid), reads=[b_sv], writes=[b_sv])
                S.add("dve", lambda e, dd=dd, ge=ge, lo=lo: e.scalar_tensor_tensor(out=lo, in0=dd, scalar=ge, in1=lo, op0=ALU.mult, op1=ALU.add), reads=[b_sv], writes=[b_sv])
                S.add("dve", lambda e, ee=ee, ge=ge, hi=hi, mid=mid: e.scalar_tensor_tensor(out=hi, in0=ee, scalar=ge, in1=mid, op0=ALU.mult, op1=ALU.add), reads=[b_sv], writes=[b_sv])
                S.add("dve", lambda e, ss=ss, lo=lo, hi=hi: e.tensor_add(out=ss, in0=lo, in1=hi), reads=[b_sv], writes=[b_sv])
                S.add("dve", lambda e, ss=ss, mid=mid: e.tensor_scalar(out=mid, in0=ss, scalar1=0.5, scalar2=None, op0=ALU.mult), reads=[b_sv], writes=[b_sv])
            S.add("dve", lambda e, av=av, n=n, lo=lo: e.tensor_scalar(out=sel[:, 0:n], in0=av, scalar1=lo, scalar2=None, op0=ALU.is_ge), reads=[b_aff, b_sv], writes=[b_sel])
            S.add("dve", lambda e, n=n: e.memset(cmp_[:, 0:n], 1.0), writes=[b_cmp])
            S.add("dve", lambda e, n=n: e.tensor_tensor_scan(out=pos[:, 0:n], data0=cmp_[:, 0:n], data1=sel[:, 0:n], initial=0.0, op0=ALU.mult, op1=ALU.add),
                  reads=[b_cmp, b_sel], writes=[b_pos])
            S.add("dve", lambda e, n=n, cap=cap: e.scalar_tensor_tensor(out=cmp_[:, 0:n], in0=pos[:, 0:n], scalar=float(cap) + 0.5, in1=sel[:, 0:n], op0=ALU.is_le, op1=ALU.mult),
                  reads=[b_pos, b_sel], writes=[b_cmp])
            S.add("dve", lambda e, n=n, soff=soff: e.tensor_scalar(out=pos[:, 0:n], in0=pos[:, 0:n], scalar1=BIGIDX + float(soff) - 1.0, scalar2=None, op0=ALU.add), reads=[b_pos], writes=[b_pos])
            S.add("dve", lambda e, n=n, c0=c0: e.scalar_tensor_tensor(out=idxf[:, c0:c0 + n], in0=cmp_[:, 0:n], scalar=-BIGIDX, in1=pos[:, 0:n], op0=ALU.mult, op1=ALU.add),
                  reads=[b_cmp, b_pos], writes=[b_idxf])
        tiles = [ti for ti in range(k.ntile) if not (last and ti * 128 < TC)]
        for b0 in range(0, len(tiles), 32):
            grp = tiles[b0:b0 + 32]
            for j, ti in enumerate(grp):
                S.add("pe", lambda e, j=j, ti=ti: e.transpose(out=pidx[:, j, :], in_=idxf[:, ti * 128:(ti + 1) * 128], identity=k.identf[0:NE, 0:NE]),
                      reads=[b_idxf, k.b_identf], writes=[b_pidx])
            S.add("act", lambda e, grp=grp: e.activation(out=idxt[:, grp[0]:grp[-1] + 1, :], in_=pidx[:, 0:len(grp), :], func=AF.Copy), reads=[b_pidx], writes=[b_idxt])
        S.add("dve", lambda e: e.scalar_tensor_tensor(out=k.gsel[:, tiles[0]:, :], in0=idxt[:, tiles[0]:, :], scalar=BIGIDX * 0.5, in1=k.afft[:, tiles[0]:, :], op0=ALU.is_lt, op1=ALU.mult),
              reads=[b_idxt, k.b_afft], writes=[k.b_gsel])
        S.add("dve", lambda e: e.tensor_tensor(out=idxt[:, tiles[0]:, :], in0=idxt[:, tiles[0]:, :], in1=k.eoff[:, tiles[0]:, :], op=ALU.add), reads=[b_idxt, k.b_eoff], writes=[b_idxt])
        S.add("dve", lambda e: e.tensor_copy(out=k.idxi[:, tiles[0]:, :], in_=idxt[:, tiles[0]:, :]), reads=[b_idxt], writes=[k.b_idxi])
        bound = NE * k.SLOTP - 1
        xs_flat = k.xs.rearrange("e s d -> (e s) d")
        for ti in tiles:
            h_, b_h = h2t.next()
            dma(k, "sp", h_[:], k.h2[ti * 128:(ti + 1) * 128, :], [k.b_h2], [b_h])
            for ex in range(NE):
                S.add("pool", lambda e, h_=h_, ti=ti, ex=ex: e.indirect_dma_start(
                    out=xs_flat, out_offset=bass.IndirectOffsetOnAxis(ap=k.idxi[:, ti, ex:ex + 1], axis=0), in_=h_[:], in_offset=None,
                    bounds_check=bound_reg(k, e, bound), oob_is_err=False), reads=[b_h, k.b_idxi], writes=[k.b_xs], dma=True, waw=False)
        S.barrier()


def pass_experts(k, l):
    nc, S = k.nc, k.S
    last = (l == k.L - 1)
    nslots = k.cap_l if last else k.SLOTS
    stiles = []
    s0 = 0
    while s0 < nslots:
        stiles.append((s0, min(128, nslots - s0)))
        s0 += 128
    sgroups = [stiles[i:i + 4] for i in range(0, len(stiles), 4)]
    with ExitStack() as es:
        wg = Ring([sbt(k, es, f"e_wg{i}", [128, 8, D], BF16) for i in range(2)])
        wu = Ring([sbt(k, es, f"e_wu{i}", [128, 8, D], BF16) for i in range(2)])
        wd = Ring([sbt(k, es, f"e_wd{i}", [128, 8, D], BF16) for i in range(2)])
        xt = Ring([sbt(k, es, f"e_x{i}", [128, D], BF16) for i in range(4)])
        xsT = Ring([sbt(k, es, f"e_xT{i}", [128, 8, 512], BF16) for i in range(2)])
        sgr = Ring([sbt(k, es, f"e_sg{i}", [128, 512], F32) for i in range(2)])
        hid = Ring([sbt(k, es, f"e_hid{i}", [128, 8, 512], BF16) for i in range(2)])
        yt = Ring([sbt(k, es, f"e_y{i}", [128, D], F32) for i in range(2)])
        pT = Ring([pst(k, es, f"e_pT{i}", [128, 8, 128], BF16) for i in range(2)])
        pg = Ring([pst(k, es, f"e_pg{i}", [128, 512]) for i in range(2)])
        pu = Ring([pst(k, es, f"e_pu{i}", [128, 512]) for i in range(2)])
        po = Ring([pst(k, es, f"e_po{i}", [128, 512]) for i in range(2)])
        for ex in range(NE):
            g_, b_g = wg.next(); u_, b_u = wu.next(); d_, b_d = wd.next()
            dma(k, "pool", g_[:], k.w_gate[l, ex].rearrange("(k p) c -> p k c", p=128), [k.b_in], [b_g])
            dma(k, "pool", u_[:], k.w_up[l, ex].rearrange("(k p) c -> p k c", p=128), [k.b_in], [b_u])
            dma(k, "pool", d_[:], k.w_down[l, ex].rearrange("(k p) c -> p k c", p=128), [k.b_in], [b_d])
            for grp in sgroups:
                xT_, b_xT = xsT.next()
                off = 0
                offs = []
                for (s0, r) in grp:
                    x_, b_x = xt.next(); p_, b_p = pT.next()
                    dma(k, "sp", x_[0:r, :], k.xs[ex, s0:s0 + r, :], [k.b_xs], [b_x])
                    for kk in range(8):
                        S.add("pe", lambda e, p_=p_, x_=x_, kk=kk, r=r: e.transpose(out=p_[:, kk, 0:r], in_=x_[0:r, kk * 128:(kk + 1) * 128], identity=k.identb[0:r, 0:r]),
                              reads=[b_x, k.b_identb], writes=[b_p])
                    S.add("act", lambda e, p_=p_, xT_=xT_, off=off, r=r: e.activation(out=xT_[:, :, off:off + r], in_=p_[:, :, 0:r], func=AF.Copy), reads=[b_p], writes=[b_xT])
                    offs.append(off)
                    off += r
                ncol = off
                h_, b_h = hid.next()
                for fc in range(8):
                    pg_, b_pg = pg.next(); pu_, b_pu = pu.next(); sg_, b_sg = sgr.next()
                    for kk in range(8):
                        S.add("pe", lambda e, pg_=pg_, g_=g_, xT_=xT_, kk=kk, fc=fc, ncol=ncol: e.matmul(pg_[:, 0:ncol], lhsT=g_[:, kk, fc * 128:(fc + 1) * 128], rhs=xT_[:, kk, 0:ncol],
                                                                                                     start=(kk == 0), stop=(kk == 7)), reads=[b_g, b_xT], writes=[b_pg])
                    for kk in range(8):
                        S.add("pe", lambda e, pu_=pu_, u_=u_, xT_=xT_, kk=kk, fc=fc, ncol=ncol: e.matmul(pu_[:, 0:ncol], lhsT=u_[:, kk, fc * 128:(fc + 1) * 128], rhs=xT_[:, kk, 0:ncol],
                                                                                                     start=(kk == 0), stop=(kk == 7)), reads=[b_u, b_xT], writes=[b_pu])
                    S.add("act", lambda e, sg_=sg_, pg_=pg_, ncol=ncol: e.activation(out=sg_[:, 0:ncol], in_=pg_[:, 0:ncol], func=AF.Silu), reads=[b_pg], writes=[b_sg])
                    S.add("dve", lambda e, h_=h_, sg_=sg_, pu_=pu_, fc=fc, ncol=ncol: e.tensor_tensor(out=h_[:, fc, 0:ncol], in0=pu_[:, 0:ncol], in1=sg_[:, 0:ncol], op=ALU.mult),
                          reads=[b_pu, b_sg], writes=[b_h])
                for (s0, r), o_ in zip(grp, offs):
                    y_, b_y = yt.next()
                    for half in range(2):
                        po_, b_po = po.next()
                        for fc in range(8):
                            S.add("pe", lambda e, po_=po_, h_=h_, d_=d_, fc=fc, o_=o_, r=r, half=half: e.matmul(po_[0:r, :], lhsT=h_[:, fc, o_:o_ + r], rhs=d_[:, fc, half * 512:(half + 1) * 512],
                                                                                                            start=(fc == 0), stop=(fc == 7)), reads=[b_h, b_d], writes=[b_po])
                        S.add("act", lambda e, po_=po_, y_=y_, r=r, half=half: e.activation(out=y_[0:r, half * 512:(half + 1) * 512], in_=po_[0:r, :], func=AF.Copy), reads=[b_po], writes=[b_y])
                    dma(k, "act", k.ys[ex, s0:s0 + r, :], y_[0:r, :], [b_y], [k.b_ys])
        S.barrier()


def pass_combine(k, l):
    nc, S = k.nc, k.S
    last = (l == k.L - 1)
    TC = k.TC
    with ExitStack() as es:
        bc = load_bc(k, es, l, ["g2"])
        lng, b_lng = load_row_bc(k, es, "c_lng", k.ln_g[l, 1:2, :])
        lnb, b_lnb = load_row_bc(k, es, "c_lnb", k.ln_b[l, 1:2, :])
        gb = [sbt(k, es, f"c_g{e_}", [128, D], F32) for e_ in range(NE)]
        acc = Ring([sbt(k, es, f"c_acc{i}", [128, D], F32) for i in range(2)])
        x1t = Ring([sbt(k, es, f"c_x1{i}", [128, D], F32) for i in range(2)])
        zt = Ring([sbt(k, es, f"c_z{i}", [128, D], F32) for i in range(2)])
        xo = Ring([sbt(k, es, f"c_xo{i}", [128, D], F32) for i in range(2)])
        stt = Ring([sbt(k, es, f"c_st{i}", [128, 12], F32) for i in range(2)])
        mvt = Ring([sbt(k, es, f"c_mv{i}", [128, 2], F32) for i in range(2)])
        rss = Ring([sbt(k, es, f"c_rs{i}", [128, 2], F32) for i in range(2)])
        for e_ in range(NE):
            S.add("dve", lambda e, e_=e_: e.memset(gb[e_][0][:], 0.0), writes=[gb[e_][1]])
        bound = NE * k.SLOTP - 1
        ys_flat = k.ys.rearrange("e s d -> (e s) d")
        tiles = [ti for ti in range(k.ntile) if not (last and ti * 128 < TC)]
        def tile_stages(ti):
            isctx = 1 if ti * 128 < TC else 0
            a_, b_a = acc.next(); x1_, b_x1 = x1t.next(); z_, b_z = zt.next(); o_, b_o = xo.next()

            def fa():
                dma(k, "sp", x1_[:], k.x1[ti * 128:(ti + 1) * 128, :], [k.b_x1], [b_x1])
                for e_ in range(NE):
                    g_, b_g = gb[e_]
                    S.add("pool", lambda e, g_=g_, e_=e_: e.indirect_dma_start(
                        out=g_[:], out_offset=None, in_=ys_flat, in_offset=bass.IndirectOffsetOnAxis(ap=k.idxi[:, ti, e_:e_ + 1], axis=0),
                        bounds_check=bound_reg(k, e, bound), oob_is_err=False), reads=[k.b_ys, k.b_idxi], writes=[b_g], dma=True, waw=False)
                    if e_ == 0:
                        S.add("dve", lambda e, g_=g_, e_=e_: e.tensor_scalar(out=a_[:], in0=g_[:], scalar1=k.gsel[:, ti, e_:e_ + 1], scalar2=None, op0=ALU.mult),
                              reads=[b_g, k.b_gsel], writes=[b_a])
                    else:
                        S.add("dve", lambda e, g_=g_, e_=e_: e.scalar_tensor_tensor(out=a_[:], in0=g_[:], scalar=k.gsel[:, ti, e_:e_ + 1], in1=a_[:], op0=ALU.mult, op1=ALU.add),
                              reads=[b_g, k.b_gsel, b_a], writes=[b_a])
                    yield

            def fb():
                g2, b_g2 = bc[("g2", isctx)]
                S.add("pool", lambda e: e.tensor_tensor(out=a_[:], in0=a_[:], in1=g2[:], op=ALU.mult), reads=[b_a, b_g2], writes=[b_a])
                yield
                S.add("dve", lambda e: e.scalar_tensor_tensor(out=z_[:], in0=x1_[:], scalar=ALPHA, in1=a_[:], op0=ALU.mult, op1=ALU.add), reads=[b_x1, b_a], writes=[b_z])
                yield
                st, b_st = stt.next(); mv, b_mv = mvt.next(); rs, b_rs = rss.next()
                S.add("dve", lambda e: e.bn_stats(out=st[:, 0:6], in_=z_[:, 0:512]), reads=[b_z], writes=[b_st])
                yield
                S.add("dve", lambda e: e.bn_stats(out=st[:, 6:12], in_=z_[:, 512:1024]), reads=[b_z], writes=[b_st])
                S.add("dve", lambda e: e.bn_aggr(out=mv[:, 0:2], in_=st[:, 0:12]), reads=[b_st], writes=[b_mv])
                S.add("act", lambda e: e.activation(out=rs[:, 0:1], in_=mv[:, 1:2], func=AF.Sqrt, bias=k.epsln[:, 0:1]), reads=[b_mv, k.b_epsln], writes=[b_rs])
                yield
                S.add("dve", lambda e: e.reciprocal(out=rs[:, 1:2], in_=rs[:, 0:1]), reads=[b_rs], writes=[b_rs])
                S.add("dve", lambda e: e.tensor_scalar(out=o_[:], in0=z_[:], scalar1=mv[:, 0:1], scalar2=rs[:, 1:2], op0=ALU.subtract, op1=ALU.mult), reads=[b_z, b_mv, b_rs], writes=[b_o])
                yield
                S.add("pool", lambda e: e.tensor_tensor(out=o_[:], in0=o_[:], in1=lng[:], op=ALU.mult), reads=[b_o, b_lng], writes=[b_o])
                yield
                S.add("dve", lambda e: e.tensor_tensor(out=o_[:], in0=o_[:], in1=lnb[:], op=ALU.add), reads=[b_o, b_lnb], writes=[b_o])
                if last:
                    r0 = ti * 128 - TC
                    dma(k, "act", k.out[r0:r0 + 128, :], o_[:], [b_o], [k.b_out])
                else:
                    dma(k, "act", k.xn[ti * 128:(ti + 1) * 128, :], o_[:], [b_o], [k.b_xn])
                yield
            return [fa, fb]

        tiles_st = []

        def run_round():
            gens = []
            for ent in list(tiles_st):
                gens.append(ent.pop(0)())
                if not ent:
                    tiles_st.remove(ent)
            while gens:
                for g in list(gens):
                    try:
                        next(g)
                    except StopIteration:
                        gens.remove(g)
        for ti in tiles:
            tiles_st.append(tile_stages(ti))
            run_round()
        while tiles_st:
            run_round()
        S.barrier()


_NC_CACHE = {}
PER_SAMPLE = ("x", "c", "ctx")


def make_in_map(inputs, b):
    m = {}
    for name, v in inputs.items():
        v = np.asarray(v)
        if name in PER_SAMPLE:
            m[name] = np.ascontiguousarray(v[b])
        else:
            m[name] = np.ascontiguousarray(v)
    return m


def kernel(**inputs):
    x = np.asarray(inputs["x"])
    B, T, _ = x.shape
    TC = np.asarray(inputs["ctx"]).shape[1]
    key = (T, TC)
    if key not in _NC_CACHE:
        _NC_CACHE[key] = build_program(T, TC)
    nc = _NC_CACHE[key]
    ncores = 8
    maps = [make_in_map(inputs, i % B) for i in range(ncores)]
    res = run_bass_kernel_spmd(nc, maps, core_ids=list(range(ncores)))
    out = np.stack([np.asarray(res.results[b]["out"]) for b in range(B)], axis=0)
    return out.astype(np.float32)
```

```python
import numpy as np
import concourse.bass as bass
import concourse.mybir as mybir
from concourse.bass_utils import run_bass_kernel_spmd
from contextlib import ExitStack

F32 = mybir.dt.float32
BF16 = mybir.dt.bfloat16
I32 = mybir.dt.int32
U32 = mybir.dt.uint32
AF = mybir.ActivationFunctionType
ALU = mybir.AluOpType
AX = mybir.AxisListType

SEM_CHUNK = 24000


class Buf:
    __slots__ = ("name", "w", "r", "dsem", "dcnt")

    def __init__(self, name):
        self.name = name
        self.w = None
        self.r = []
        self.dsem = None
        self.dcnt = 0


class Sched:
    ENG = ("pe", "act", "dve", "pool", "sp")

    def __init__(self, nc):
        self.nc = nc
        self.ops = {e: [] for e in self.ENG}
        self.dsems = []
        self.seen_c = {e: {} for e in self.ENG}
        self.seen_d = {e: {} for e in self.ENG}
        self.nbuf = 0
        self._dtot = {}
        self.NQ = {"sp": 40, "pool": 24, "act": 16}
        self.dma_base = {"sp": 0, "pool": 40, "act": 64}
        self.dma_n = {"sp": 0, "pool": 0, "act": 0}
        self.dsems = [None] * 80

    def buf(self, name=None):
        self.nbuf += 1
        return Buf(name or f"b{self.nbuf}")

    def _need(self, eng, tok, waits):
        if tok is None:
            return
        if tok[0] == "c":
            _, e, idx = tok
            if e == "pe" and eng == "pe":
                return
            if self.seen_c[eng].get(e, -1) >= idx:
                return
            if e == eng:
                pass
            self.seen_c[eng][e] = idx
            self.ops[e][idx]["signal"] = True
            waits.append(tok)
        else:
            _, slot, val = tok
            if self.seen_d[eng].get(slot, -1) >= val:
                return
            self.seen_d[eng][slot] = val
            waits.append(tok)

    def add(self, eng, fn, reads=(), writes=(), dma=False, waw=True):
        waits = []
        for b in reads:
            self._need(eng, b.w, waits)
        for b in writes:
            if waw:
                self._need(eng, b.w, waits)
            for t in b.r:
                self._need(eng, t, waits)
        idx = len(self.ops[eng])
        rec = {"fn": fn, "waits": waits, "signal": False, "dma": None}
        if dma:
            nq = self.NQ[eng]
            i = self.dma_n[eng]
            self.dma_n[eng] = i + 1
            slot = self.dma_base[eng] + (i % nq)
            val = 16 * (i // nq + 1)
            if val > 16:
                self._need(eng, ("d", slot, val - 16), waits)
            self._dtot[slot] = val
            tok = ("d", slot, val)
            rec["dma"] = slot
        else:
            tok = ("c", eng, idx)
        self.ops[eng].append(rec)
        for b in reads:
            b.r.append(tok)
            if len(b.r) > 12:
                b.r = self._compact(b.r)
        for b in writes:
            b.w = tok
            b.r = []
        return tok

    @staticmethod
    def _compact(toks):
        best = {}
        for t in toks:
            key = (t[0], t[1])
            if key not in best or best[key][2] < t[2]:
                best[key] = t
        return list(best.values())

    def barrier(self):
        last_c = {}
        for e in self.ENG:
            for i in range(len(self.ops[e]) - 1, -1, -1):
                if self.ops[e][i]["dma"] is None and self.ops[e][i]["fn"] is not None:
                    last_c[e] = i
                    break
        for e in self.ENG:
            waits = []
            for src, idx in last_c.items():
                self._need(e, ("c", src, idx), waits)
            for slot, val in self._dma_totals().items():
                self._need(e, ("d", slot, val), waits)
            if waits:
                self.ops[e].append({"fn": None, "waits": waits, "signal": False, "dma": None})

    def _dma_totals(self):
        return dict(self._dtot)

    def emit(self):
        nc = self.nc
        ndsem = len(self.dsems)
        dsem_h = [nc.alloc_semaphore(f"dq{i}") for i in range(ndsem)]
        csem = {}
        for e in self.ENG:
            n = 0
            for rec in self.ops[e]:
                if rec["dma"] is None and rec["signal"]:
                    rec["ord"] = n
                    n += 1
            nep = (n + SEM_CHUNK - 1) // SEM_CHUNK
            csem[e] = [nc.alloc_semaphore(f"c_{e}{k}") for k in range(max(nep, 1))]

        def run(e, engobj):
            for rec in self.ops[e]:
                for t in rec["waits"]:
                    if t[0] == "c":
                        o = self.ops[t[1]][t[2]]["ord"]
                        engobj.wait_ge(csem[t[1]][o // SEM_CHUNK], o % SEM_CHUNK + 1)
                    else:
                        engobj.wait_ge(dsem_h[t[1]], t[2])
                if rec["fn"] is None:
                    continue
                ins = rec["fn"](engobj)
                if rec["dma"] is not None:
                    ins.then_inc(dsem_h[rec["dma"]], 16)
                elif rec["signal"]:
                    o = rec["ord"]
                    ins.then_inc(csem[e][o // SEM_CHUNK], 1)

        with nc.Block() as block:
            @block.tensor
            def _(pe):
                run("pe", pe)

            @block.scalar
            def _(act):
                run("act", act)

            @block.vector
            def _(dve):
                run("dve", dve)

            @block.gpsimd
            def _(pool):
                run("pool", pool)

            @block.sync
            def _(sp):
                run("sp", sp)


D = 1024
NH = 8
CH = 64
GW = 64
NE = 16
LN_EPS = 1e-5
RMS_EPS = 1e-6
ALPHA = 4.0 ** 0.25
QSCALE = 128.0 ** -0.5
BIGIDX = 1.0e6


class Ring:
    def __init__(self, items):
        self.items = items
        self.i = 0

    def next(self):
        it = self.items[self.i % len(self.items)]
        self.i += 1
        return it


class K:
    pass


def build_program(T, TC, L=2, debug=False, nrun=None):
    nc = bass.Bass("TRN2", target_bir_lowering=False)
    k = K()
    k.nc, k.T, k.TC, k.L = nc, T, TC, L
    k.debug = bool(debug)
    TA = T + TC
    k.TA = TA
    k.ROWS = T // GW
    k.cap_l = 2 * T // NE
    k.cap_c = 2 * TC // NE
    k.SLOTS = k.cap_l + k.cap_c
    k.NST = (k.SLOTS + 127) // 128
    k.SLOTP = k.NST * 128
    S = Sched(nc)
    k.S = S

    def din(name, shape, dt=F32):
        return nc.dram_tensor(name, list(shape), dt, kind="ExternalInput").ap()

    def dscr(name, shape, dt=F32):
        if debug:
            return nc.dram_tensor(name, list(shape), dt, kind="ExternalOutput").ap()
        return nc.dram_tensor(name, list(shape), dt).ap()

    k.x = din("x", [T, D]); k.ctx = din("ctx", [TC, D])
    k.c = din("c", [D]); k.c_ctx = din("c_ctx", [D])
    k.w_mod = din("w_mod", [L, D, 6 * D]); k.b_mod = din("b_mod", [L, 6 * D])
    k.w_in = din("w_in", [L, D, 9 * D])
    k.lb_logits = din("hgrn_lb_logits", [L, 2, D]); k.norm_g = din("hgrn_norm_g", [L, 128])
    k.conv_w = din("conv_w", [L, 4, D]); k.conv_b = din("conv_b", [L, D])
    k.lru_wa = din("lru_wa", [L, 2, 8, 128, 128]); k.lru_ba = din("lru_ba", [L, 2, D])
    k.lru_wx = din("lru_wx", [L, 2, 8, 128, 128]); k.lru_bx = din("lru_bx", [L, 2, D])
    k.lru_lam = din("lru_lambda", [L, 2, D])
    k.w_ba = din("w_branch_a", [L, D, D]); k.w_bb = din("w_branch_b", [L, D, D]); k.w_out = din("w_out", [L, D, D])
    k.ln_g = din("ln_g", [L, 2, D]); k.ln_b = din("ln_b", [L, 2, D])
    k.w_router = din("w_router", [L, D, NE])
    k.w_gate = din("w_gate", [L, NE, D, D]); k.w_up = din("w_up", [L, NE, D, D]); k.w_down = din("w_down", [L, NE, D, D])
    k.out = nc.dram_tensor("out", [T, D], F32, kind="ExternalOutput").ap()
    k.b_in = S.buf("inputs")
    k.b_out = S.buf("out")

    def scr(name, shape, dt=F32):
        ap = dscr(name, shape, dt)
        return ap, S.buf(name)
    k.modv, k.b_modv = scr("modv", [L, 2, 6 * D])
    k.hT, k.b_hT = scr("hT", [D, TA], BF16)
    k.qs, k.b_qs = scr("qs", [D, TA], BF16)
    k.vtm, k.b_vtm = scr("vtm", [TA, D], BF16)
    k.lf = [None, None]; k.b_lf = [None, None]; k.kk = [None, None]; k.b_kk = [None, None]
    for d_ in range(2):
        k.lf[d_], k.b_lf[d_] = scr(f"lf{d_}", [D, TA])
        k.kk[d_], k.b_kk[d_] = scr(f"kk{d_}", [D, TA])
    k.sog, k.b_sog = scr("sog", [D, TA], BF16)
    k.lx, k.b_lx = scr("lx", [D, TA])
    k.gly, k.b_gly = scr("gly", [D, TA], BF16)
    k.sma, k.b_sma = scr("sma", [D, TA], BF16)
    k.smb, k.b_smb = scr("smb", [D, TA], BF16)
    k.od = [None, None]; k.b_od = [None, None]
    for d_ in range(2):
        k.od[d_], k.b_od[d_] = scr(f"od{d_}", [D, TA])
    k.gT, k.b_gT = scr("gT", [D, TA], BF16)
    k.x1, k.b_x1 = scr("x1", [TA, D])
    k.h2, k.b_h2 = scr("h2", [TA, D], BF16)
    k.xs, k.b_xs = scr("xs", [NE, k.SLOTP, D], BF16)
    k.ys, k.b_ys = scr("ys", [NE, k.SLOTP, D])
    k.xn, k.b_xn = scr("xn", [TA, D])
    k.affT, k.b_affT = scr("affT", [NE, TA])

    k.groups = [(0, TC)] + [(TC + i * 512, 512) for i in range(T // 512)]
    k.ntile = TA // 128

    with ExitStack() as es0:
        k.es0 = es0
        setup_consts(k)
        for l in range(L if nrun is None else nrun):
            last = (l == L - 1)
            pass_mod(k, l)
            pass_h(k, l)
            pass_inproj(k, l)
            pass_hgrn(k, l)
            pass_lru(k, l)
            pass_merge(k, l)
            pass_route(k, l)
            pass_experts(k, l)
            pass_combine(k, l)
        S.barrier()
        S.emit()
    return nc


_UID = [0]


def sbt(k, es, name, shape, dt):
    _UID[0] += 1
    name = f"{name}_{_UID[0]}"
    t = es.enter_context(k.nc.sbuf_tensor(name, list(shape), dt))
    return t, k.S.buf(name)


def pst(k, es, name, shape, dt=F32):
    _UID[0] += 1
    name = f"{name}_{_UID[0]}"
    t = es.enter_context(k.nc.psum_tensor(name, list(shape), dt))
    return t, k.S.buf(name)


STORE_Q = "act"
LRU_GN = 1024
HG_NI = 4
LRU_GN = 1024
SCALE_FUNC = AF.Copy


def dma(k, eng, out, in_, reads, writes, **kw):
    if eng == "act":
        eng = STORE_Q
    return k.S.add(eng, lambda e: e.dma_start(out=out, in_=in_, **kw), reads=reads, writes=writes, dma=True, waw=False)


def setup_consts(k):
    nc, S, es = k.nc, k.S, k.es0
    L = k.L
    k.identf, k.b_identf = sbt(k, es, "identf", [128, 128], F32)
    k.identb, k.b_identb = sbt(k, es, "identb", [128, 128], BF16)
    k.onesm, k.b_onesm = sbt(k, es, "onesm", [128, 128], F32)
    k.maskF, k.b_maskF = sbt(k, es, "maskF", [64, 8, 64], BF16)
    k.maskB, k.b_maskB = sbt(k, es, "maskB", [64, 8, 64], BF16)
    k.cmask, k.b_cmask = sbt(k, es, "cmask", [128, 8, 64], F32)
    with ExitStack() as es1:
        tmp, b_tmp = sbt(k, es1, "ctmp", [128, 8, 64], F32)
        S.add("pool", lambda e: e.memset(k.identf[:], 1.0), writes=[k.b_identf])
        S.add("pool", lambda e: e.affine_select(out=k.identf[:], in_=k.identf[:], pattern=[[-1, 128]], compare_op=ALU.is_equal, fill=0.0, base=0, channel_multiplier=1), reads=[k.b_identf], writes=[k.b_identf])
        S.add("dve", lambda e: e.tensor_copy(out=k.identb[:], in_=k.identf[:]), reads=[k.b_identf], writes=[k.b_identb])
        S.add("pool", lambda e: e.memset(k.onesm[:], 1.0 / 128.0), writes=[k.b_onesm])
        S.add("pool", lambda e: e.memset(tmp[:], 1.0), writes=[b_tmp])
        S.add("pool", lambda e: e.affine_select(out=tmp[0:64], in_=tmp[0:64], pattern=[[0, 8], [1, 64]], compare_op=ALU.is_ge, fill=0.0, base=0, channel_multiplier=-1), reads=[b_tmp], writes=[b_tmp])
        S.add("dve", lambda e: e.tensor_copy(out=k.maskF[:], in_=tmp[0:64]), reads=[b_tmp], writes=[k.b_maskF])
        S.add("pool", lambda e: e.memset(tmp[:], 1.0), writes=[b_tmp])
        S.add("pool", lambda e: e.affine_select(out=tmp[0:64], in_=tmp[0:64], pattern=[[0, 8], [-1, 64]], compare_op=ALU.is_ge, fill=0.0, base=0, channel_multiplier=1), reads=[b_tmp], writes=[b_tmp])
        S.add("dve", lambda e: e.tensor_copy(out=k.maskB[:], in_=tmp[0:64]), reads=[b_tmp], writes=[k.b_maskB])
        S.add("pool", lambda e: e.memset(k.cmask[:], 1.0), writes=[k.b_cmask])
        S.add("pool", lambda e: e.affine_select(out=k.cmask[:], in_=k.cmask[:], pattern=[[0, 8], [1, 64]], compare_op=ALU.is_gt, fill=0.0, base=0, channel_multiplier=0), reads=[k.b_cmask], writes=[k.b_cmask])
        S.barrier()
    def cols(name, src, n):
        t, b = sbt(k, es, name, [128, n], F32)
        dma(k, "sp", t[:], src.rearrange("n p -> p n"), [k.b_in], [b], allow_slow_non_contiguous=True)
        return t, b
    k.lbl, k.b_lbl = cols("lbl", k.lb_logits.rearrange("l d (h p) -> (l d h) p", p=128), L * 2 * 8)
    k.ng, k.b_ng = cols("ng", k.norm_g, L)
    k.cw, k.b_cw = cols("cw", k.conv_w.rearrange("l j (n p) -> (l j n) p", p=128), L * 4 * 8)
    k.cb, k.b_cb = cols("cb", k.conv_b.rearrange("l (n p) -> (l n) p", p=128), L * 8)
    k.ba, k.b_ba = cols("ba", k.lru_ba.rearrange("l d (n p) -> (l d n) p", p=128), L * 2 * 8)
    k.bx, k.b_bx = cols("bx", k.lru_bx.rearrange("l d (n p) -> (l d n) p", p=128), L * 2 * 8)
    k.lam, k.b_lam = cols("lam", k.lru_lam.rearrange("l d (n p) -> (l d n) p", p=128), L * 2 * 8)
    n = L * 2 * 8
    k.lb, k.b_lb = sbt(k, es, "lb", [128, n], F32)
    k.oml, k.b_oml = sbt(k, es, "oml", [128, n], F32)
    k.noml, k.b_noml = sbt(k, es, "noml", [128, n], F32)
    k.asc, k.b_asc = sbt(k, es, "asc", [128, n], F32)
    assert L == 2
    S.add("dve", lambda e: e.memset(k.lb[:], 0.0), writes=[k.b_lb])
    S.add("dve", lambda e: e.tensor_sub(out=k.lb[:, 16:32], in0=k.lbl[:, 16:32], in1=k.lbl[:, 0:16]), reads=[k.b_lbl, k.b_lb], writes=[k.b_lb])
    S.add("act", lambda e: e.activation(out=k.lb[:, 16:32], in_=k.lb[:, 16:32], func=AF.Sigmoid), reads=[k.b_lb], writes=[k.b_lb])
    S.add("dve", lambda e: e.tensor_scalar(out=k.oml[:], in0=k.lb[:], scalar1=-1.0, scalar2=1.0, op0=ALU.mult, op1=ALU.add), reads=[k.b_lb], writes=[k.b_oml])
    S.add("dve", lambda e: e.tensor_scalar(out=k.noml[:], in0=k.oml[:], scalar1=-1.0, scalar2=None, op0=ALU.mult), reads=[k.b_oml], writes=[k.b_noml])
    S.add("act", lambda e: e.activation(out=k.asc[:], in_=k.lam[:], func=AF.Exp, scale=-1.0), reads=[k.b_lam], writes=[k.b_asc])
    S.add("act", lambda e: e.activation(out=k.asc[:], in_=k.asc[:], func=AF.Ln, bias=1.0), reads=[k.b_asc], writes=[k.b_asc])
    S.add("dve", lambda e: e.tensor_scalar(out=k.asc[:], in0=k.asc[:], scalar1=-8.0, scalar2=None, op0=ALU.mult), reads=[k.b_asc], writes=[k.b_asc])
    k.idxi, k.b_idxi = sbt(k, es, "idxi", [128, k.ntile, NE], I32)
    k.gsel, k.b_gsel = sbt(k, es, "gsel", [128, k.ntile, NE], F32)
    k.afft, k.b_afft = sbt(k, es, "afft", [128, k.ntile, NE], F32)
    k.eoff, k.b_eoff = sbt(k, es, "eoff", [128, k.ntile, NE], F32)
    with ExitStack() as es2:
        eo_i, b_eo_i = sbt(k, es2, "eoff_i", [128, k.ntile, NE], I32)
        S.add("pool", lambda e: e.iota(eo_i[:], pattern=[[0, k.ntile], [k.SLOTP, NE]], base=0, channel_multiplier=0), writes=[b_eo_i])
        S.add("dve", lambda e: e.tensor_copy(out=k.eoff[:], in_=eo_i[:]), reads=[b_eo_i], writes=[k.b_eoff])
        S.barrier()
    k.epsln, k.b_epsln = sbt(k, es, "epsln", [128, 1], F32)
    k.epsrms, k.b_epsrms = sbt(k, es, "epsrms", [128, 1], F32)
    S.add("dve", lambda e: e.memset(k.epsln[:], LN_EPS), writes=[k.b_epsln])
    S.add("dve", lambda e: e.memset(k.epsrms[:], RMS_EPS), writes=[k.b_epsrms])


def tile_src(k, l, ti):
    if l == 0:
        t0 = ti * 128
        if t0 < k.TC:
            return k.ctx[t0:t0 + 128, :], k.b_in
        return k.x[t0 - k.TC:t0 - k.TC + 128, :], k.b_in
    return k.xn[ti * 128:(ti + 1) * 128, :], k.b_xn


def load_bc(k, es, l, names):
    idx = {"sh1": 0, "sc1": 1, "g1": 2, "sh2": 3, "sc2": 4, "g2": 5}
    res = {}
    for nm in names:
        for isctx in (0, 1):
            t, b = sbt(k, es, f"bc_{nm}{isctx}", [128, D], F32)
            src = k.modv[l, isctx:isctx + 1, idx[nm] * D:(idx[nm] + 1) * D].partition_broadcast(128)
            dma(k, "sp", t[:], src, [k.b_modv], [b])
            if nm in ("sc1", "sc2"):
                k.S.add("pool", lambda e, t=t: e.tensor_scalar(out=t[:], in0=t[:], scalar1=1.0, scalar2=None, op0=ALU.add), reads=[b], writes=[b])
            res[(nm, isctx)] = (t, b)
    return res


def load_row_bc(k, es, name, src_row):
    t, b = sbt(k, es, name, [128, D], F32)
    dma(k, "sp", t[:], src_row.partition_broadcast(128), [k.b_in], [b])
    return t, b


def pass_mod(k, l):
    nc, S = k.nc, k.S
    with ExitStack() as es:
        cc, b_cc = sbt(k, es, "m_cc", [128, 8, 2], F32)
        sc, b_sc = sbt(k, es, "m_sc", [128, 8, 2], F32)
        bm, b_bm = sbt(k, es, "m_bm", [2, 6 * D], F32)
        ms, b_ms = sbt(k, es, "m_ms", [2, 6 * D], F32)
        ws = [sbt(k, es, f"m_w{i}", [128, 8, 512], F32) for i in range(2)]
        pp = [pst(k, es, f"m_p{i}", [2, 512]) for i in range(2)]
        dma(k, "sp", cc[:, :, 0], k.c.rearrange("(k p) -> p k", p=128), [k.b_in], [b_cc], allow_slow_non_contiguous=True)
        S.add("sp", lambda e: e.dma_start(out=cc[:, :, 1], in_=k.c_ctx.rearrange("(k p) -> p k", p=128), allow_slow_non_contiguous=True),
              reads=[k.b_in], writes=[b_cc], dma=True, waw=True)
        dma(k, "sp", bm[:], k.b_mod[l:l + 1, :].partition_broadcast(2), [k.b_in], [b_bm])
        S.add("act", lambda e: e.activation(out=sc[:], in_=cc[:], func=AF.Silu), reads=[b_cc], writes=[b_sc])
        for cg in range(12):
            w, b_w = ws[cg % 2]
            p, b_p = pp[cg % 2]
            dma(k, "sp", w[:], k.w_mod[l, :, cg * 512:(cg + 1) * 512].rearrange("(k p) c -> p k c", p=128), [k.b_in], [b_w])
            for kk in range(8):
                S.add("pe", lambda e, w=w, p=p, kk=kk: e.matmul(p[:], lhsT=sc[:, kk, :], rhs=w[:, kk, :], start=(kk == 0), stop=(kk == 7)),
                      reads=[b_sc, b_w], writes=[b_p])
            S.add("dve", lambda e, p=p, cg=cg: e.tensor_tensor(out=ms[:, cg * 512:(cg + 1) * 512], in0=p[:], in1=bm[:, cg * 512:(cg + 1) * 512], op=ALU.add),
                  reads=[b_p, b_bm], writes=[b_ms])
        dma(k, "act", k.modv[l], ms[:], [b_ms], [k.b_modv])
        S.barrier()


def pass_h(k, l):
    nc, S = k.nc, k.S
    with ExitStack() as es:
        bc = load_bc(k, es, l, ["sc1", "sh1"])
        xt = Ring([sbt(k, es, f"h_x{i}", [128, D], F32) for i in range(3)])
        tt = Ring([sbt(k, es, f"h_t{i}", [128, D], F32) for i in range(2)])
        hb = Ring([sbt(k, es, f"h_hb{i}", [128, D], BF16) for i in range(2)])
        ht = Ring([sbt(k, es, f"h_ht{i}", [128, 8, 128], BF16) for i in range(2)])
        pp = Ring([pst(k, es, f"h_p{i}", [128, 8, 128], BF16) for i in range(2)])
        hTv = k.hT.rearrange("(k p) t -> p k t", p=128)
        for ti in range(k.ntile):
            isctx = 1 if ti * 128 < k.TC else 0
            src, b_src = tile_src(k, l, ti)
            x_, b_x = xt.next(); t_, b_t = tt.next(); h_, b_h = hb.next(); o_, b_o = ht.next(); p_, b_p = pp.next()
            scp, b_scp = bc[("sc1", isctx)]; sh, b_sh = bc[("sh1", isctx)]
            dma(k, "sp", x_[:], src, [b_src], [b_x])
            S.add("dve", lambda e, t_=t_, x_=x_, scp=scp: e.tensor_tensor(out=t_[:], in0=x_[:], in1=scp[:], op=ALU.mult), reads=[b_x, b_scp], writes=[b_t])
            S.add("pool", lambda e, t_=t_, h_=h_, sh=sh: e.tensor_tensor(out=h_[:], in0=t_[:], in1=sh[:], op=ALU.add), reads=[b_t, b_sh], writes=[b_h])
            for kk in range(8):
                S.add("pe", lambda e, p_=p_, h_=h_, kk=kk: e.transpose(out=p_[:, kk, :], in_=h_[:, kk * 128:(kk + 1) * 128], identity=k.identb[:]),
                      reads=[b_h, k.b_identb], writes=[b_p])
            S.add("act", lambda e, o_=o_, p_=p_: e.activation(out=o_[:], in_=p_[:], func=AF.Copy), reads=[b_p], writes=[b_o])
            dma(k, "act", hTv[:, :, ti * 128:(ti + 1) * 128], o_[:], [b_o], [k.b_hT])
        S.barrier()


FAM = ["q", "v", "ff", "fb", "og", "lx", "ly", "ma", "mb"]


def pass_inproj(k, l):
    nc, S = k.nc, k.S
    with ExitStack() as es:
        wf = Ring([sbt(k, es, f"p_w{i}", [128, 8, D], BF16) for i in range(2)])
        hg = Ring([sbt(k, es, f"p_h{i}", [128, 8, 512], BF16) for i in range(2)])
        pp = Ring([pst(k, es, f"p_p{i}", [128, 512]) for i in range(4)])
        o32 = Ring([sbt(k, es, f"p_o32_{i}", [128, 8, 512], F32) for i in range(2)])
        o32b = Ring([sbt(k, es, f"p_o32b_{i}", [128, 8, 512], F32) for i in range(2)])
        sg = Ring([sbt(k, es, f"p_sg{i}", [128, 8, 512], F32) for i in range(1)])
        o16 = Ring([sbt(k, es, f"p_o16_{i}", [128, 8, 512], BF16) for i in range(2)])
        vt = Ring([sbt(k, es, f"p_vt{i}", [128, D], BF16) for i in range(2)])
        hTv = k.hT.rearrange("(k p) t -> p k t", p=128)
        win = k.w_in[l].rearrange("(k p) c -> p k c", p=128)

        def fm_view(ap, t0, n):
            return ap.rearrange("(c p) t -> p c t", p=128)[:, :, t0:t0 + n]

        for fi, fam in enumerate(FAM):
            w, b_w = wf.next()
            dma(k, "pool", w[:], win[:, :, fi * D:(fi + 1) * D], [k.b_in], [b_w])
            for (t0, n) in k.groups:
                h_, b_h = hg.next()
                dma(k, "sp", h_[:, :, 0:n], hTv[:, :, t0:t0 + n], [k.b_hT], [b_h])
                if fam == "v":
                    for tt in range(n // 128):
                        v_, b_v = vt.next()
                        for half in range(2):
                            p_, b_p = pp.next()
                            for kk in range(8):
                                S.add("pe", lambda e, p_=p_, h_=h_, w=w, kk=kk, tt=tt, half=half: e.matmul(
                                    p_[:, :], lhsT=h_[:, kk, tt * 128:(tt + 1) * 128], rhs=w[:, kk, half * 512:(half + 1) * 512],
                                    start=(kk == 0), stop=(kk == 7)), reads=[b_h, b_w], writes=[b_p])
                            S.add("act", lambda e, p_=p_, v_=v_, half=half: e.activation(out=v_[:, half * 512:(half + 1) * 512], in_=p_[:, :], func=AF.Copy),
                                  reads=[b_p], writes=[b_v])
                        dma(k, "act", k.vtm[t0 + tt * 128:t0 + (tt + 1) * 128, :], v_[:], [b_v], [k.b_vtm])
                    continue
                if fam in ("ff", "fb"):
                    d_ = 0 if fam == "ff" else 1
                    s_, b_s = sg.next(); lfo, b_lfo = o32.next(); ko, b_ko = o32b.next()
                    for cc in range(8):
                        p_, b_p = pp.next()
                        for kk in range(8):
                            S.add("pe", lambda e, p_=p_, h_=h_, w=w, kk=kk, cc=cc, n=n: e.matmul(
                                p_[:, 0:n], lhsT=w[:, kk, cc * 128:(cc + 1) * 128], rhs=h_[:, kk, 0:n], start=(kk == 0), stop=(kk == 7)),
                                reads=[b_h, b_w], writes=[b_p])
                        S.add("act", lambda e, p_=p_, s_=s_, cc=cc, n=n: e.activation(out=s_[:, cc, 0:n], in_=p_[:, 0:n], func=AF.Sigmoid), reads=[b_p], writes=[b_s])
                    for cc in range(8):
                        col = (l * 2 + d_) * 8 + cc
                        S.add("act", lambda e, s_=s_, lfo=lfo, cc=cc, n=n, col=col: e.activation(
                            out=lfo[:, cc, 0:n], in_=s_[:, cc, 0:n], func=AF.Ln, scale=k.oml[:, col:col + 1], bias=k.lb[:, col:col + 1]),
                            reads=[b_s, k.b_oml, k.b_lb], writes=[b_lfo])
                        S.add("dve", lambda e, s_=s_, ko=ko, cc=cc, n=n, col=col: e.tensor_scalar(
                            out=ko[:, cc, 0:n], in0=s_[:, cc, 0:n], scalar1=k.noml[:, col:col + 1], scalar2=k.oml[:, col:col + 1], op0=ALU.mult, op1=ALU.add),
                            reads=[b_s, k.b_oml, k.b_noml], writes=[b_ko])
                    dma(k, "act", fm_view(k.lf[d_], t0, n), lfo[:, :, 0:n], [b_lfo], [k.b_lf[d_]])
                    dma(k, "act", fm_view(k.kk[d_], t0, n), ko[:, :, 0:n], [b_ko], [k.b_kk[d_]])
                    continue
                func = {"q": AF.Silu, "og": AF.Silu, "lx": AF.Copy, "ly": AF.Gelu, "ma": AF.Sigmoid, "mb": AF.Sigmoid}[fam]
                if fam == "lx":
                    o_, b_o = o32.next(); dst, b_dst = k.lx, k.b_lx
                else:
                    o_, b_o = o16.next()
                    dst, b_dst = {"q": (k.qs, k.b_qs), "og": (k.sog, k.b_sog), "ly": (k.gly, k.b_gly), "ma": (k.sma, k.b_sma), "mb": (k.smb, k.b_smb)}[fam]
                for cc in range(8):
                    p_, b_p = pp.next()
                    for kk in range(8):
                        S.add("pe", lambda e, p_=p_, h_=h_, w=w, kk=kk, cc=cc, n=n: e.matmul(
                            p_[:, 0:n], lhsT=w[:, kk, cc * 128:(cc + 1) * 128], rhs=h_[:, kk, 0:n], start=(kk == 0), stop=(kk == 7)),
                            reads=[b_h, b_w], writes=[b_p])
                    S.add("act", lambda e, p_=p_, o_=o_, cc=cc, n=n, func=func: e.activation(out=o_[:, cc, 0:n], in_=p_[:, 0:n], func=func), reads=[b_p], writes=[b_o])
                dma(k, "act", fm_view(dst, t0, n), o_[:, :, 0:n], [b_o], [b_dst])
        S.barrier()


def pass_hgrn(k, l):
    nc, S = k.nc, k.S
    TC, T = k.TC, k.T
    with ExitStack() as es:
        lfr = Ring([sbt(k, es, f"g_lf{i}", [128, 512], F32) for i in range(2)])
        kr = Ring([sbt(k, es, f"g_k{i}", [128, 512], F32) for i in range(2)])
        qr = Ring([sbt(k, es, f"g_q{i}", [128, 512], BF16) for i in range(2)])
        br = Ring([sbt(k, es, f"g_b{i}", [128, 512], F32) for i in range(2)])
        cr = Ring([sbt(k, es, f"g_c{i}", [128, 512], F32) for i in range(2)])
        e1r = Ring([sbt(k, es, f"g_e1{i}", [128, 512], F32) for i in range(2)])
        e2r = Ring([sbt(k, es, f"g_e2{i}", [128, 512], F32) for i in range(2)])
        def hset(nm, shape, dt):
            return [[sbt(k, es, f"g_{nm}{p}_{h}", shape, dt) for h in range(NH)] for p in range(2)]
        qt = hset("qt", [128, 512], BF16); kt = hset("kt", [128, 512], BF16)
        ktm = hset("ktm", [64, 8, 128], BF16); vtm = hset("vtm", [64, 8, 128], BF16)
        attm = hset("att", [64, 512], BF16); eend = hset("ee", [128, 8], F32)
        s32 = [sbt(k, es, f"g_s32{h}", [128, 128], F32) for h in range(NH)]
        r32 = [sbt(k, es, f"g_r32{h}", [128, 128], F32) for h in range(NH)]
        sbf = [sbt(k, es, f"g_sbf{h}", [128, 128], BF16) for h in range(NH)]
        osb = Ring([sbt(k, es, f"g_o{i}", [128, 512], F32) for i in range(2)])
        pt = Ring([pst(k, es, f"g_pt{i}", [64, 8, 128], BF16) for i in range(1)])
        pa = Ring([pst(k, es, f"g_pa{i}", [64, 512]) for i in range(2)])
        po = [pst(k, es, f"g_po{i}", [128, 512]) for i in range(2)]
        psd = [pst(k, es, f"g_psd{i}", [128, 128]) for i in range(2)]
        cm = k.cmask[:].rearrange("p c t -> p (c t)")

        def prep_gen(d_, t0, n, par):
            nch = n // CH
            for h in range(NH):
                lf_, b_lf = lfr.next(); k_, b_k = kr.next(); q_, b_q = qr.next(); b_, b_b = br.next()
                e1, b_e1 = e1r.next(); e2, b_e2 = e2r.next()
                rows = slice(h * 128, (h + 1) * 128)
                dma(k, "sp", lf_[:, 0:n], k.lf[d_][rows, t0:t0 + n], [k.b_lf[d_]], [b_lf])
                dma(k, "sp", k_[:, 0:n], k.kk[d_][rows, t0:t0 + n], [k.b_kk[d_]], [b_k])
                dma(k, "sp", q_[:, 0:n], k.qs[rows, t0:t0 + n], [k.b_qs], [b_q])
                v_, b_v = vtm[par][h]
                dma(k, "sp", v_[:, 0:nch, :], k.vtm[t0:t0 + n, rows].rearrange("(c s) v -> s c v", s=CH), [k.b_vtm], [b_v])
                yield
                S.add("dve", lambda e, b_=b_, lf_=lf_, n=n: e.tensor_tensor_scan(
                    out=b_[:, 0:n], data0=cm[:, 0:n], data1=lf_[:, 0:n], initial=0.0, op0=ALU.mult, op1=ALU.add),
                    reads=[b_lf, k.b_cmask], writes=[b_b])
                yield
                ee, b_ee = eend[par][h]
                qt_, b_qt = qt[par][h]; kt_, b_kt = kt[par][h]
                if d_ == 0:
                    S.add("act", lambda e, e1=e1, b_=b_, n=n: e.activation(out=e1[:, 0:n], in_=b_[:, 0:n], func=AF.Exp), reads=[b_b], writes=[b_e1])
                    S.add("act", lambda e, e2=e2, b_=b_, n=n: e.activation(out=e2[:, 0:n], in_=b_[:, 0:n], func=AF.Exp, scale=-1.0), reads=[b_b], writes=[b_e2])
                    yield
                    S.add("act", lambda e, ee=ee, e1=e1, n=n, nch=nch: e.activation(
                        out=ee[:, 0:nch], in_=e1[:, 0:n].rearrange("p (c t) -> p c t", t=CH)[:, :, CH - 1], func=AF.Copy), reads=[b_e1], writes=[b_ee])
                else:
                    c_, b_c = cr.next()
                    S.add("dve", lambda e, c_=c_, b_=b_, lf_=lf_, n=n: e.tensor_sub(out=c_[:, 0:n], in0=b_[:, 0:n], in1=lf_[:, 0:n]), reads=[b_b, b_lf], writes=[b_c])
                    yield
                    S.add("act", lambda e, e1=e1, c_=c_, n=n: e.activation(out=e1[:, 0:n], in_=c_[:, 0:n], func=AF.Exp, scale=-1.0), reads=[b_c], writes=[b_e1])
                    S.add("act", lambda e, e2=e2, c_=c_, n=n: e.activation(out=e2[:, 0:n], in_=c_[:, 0:n], func=AF.Exp), reads=[b_c], writes=[b_e2])
                    S.add("act", lambda e, ee=ee, b_=b_, n=n, nch=nch: e.activation(
                        out=ee[:, 0:nch], in_=b_[:, 0:n].rearrange("p (c t) -> p c t", t=CH)[:, :, CH - 1], func=AF.Exp), reads=[b_b], writes=[b_ee])
                yield
                S.add("dve", lambda e, qt_=qt_, q_=q_, e1=e1, n=n: e.tensor_tensor(out=qt_[:, 0:n], in0=q_[:, 0:n], in1=e1[:, 0:n], op=ALU.mult), reads=[b_q, b_e1], writes=[b_qt])
                S.add("pool", lambda e, kt_=kt_, k_=k_, e2=e2, n=n: e.tensor_tensor(out=kt_[:, 0:n], in0=k_[:, 0:n], in1=e2[:, 0:n], op=ALU.mult), reads=[b_k, b_e2], writes=[b_kt])
                yield
                p_, b_p = pt.next()
                for ci in range(nch):
                    S.add("pe", lambda e, p_=p_, kt_=kt_, ci=ci: e.transpose(out=p_[:, ci, :], in_=kt_[:, ci * CH:(ci + 1) * CH], identity=k.identb[:]),
                          reads=[b_kt, k.b_identb], writes=[b_p])
                    if ci % 4 == 3:
                        yield
                km, b_km = ktm[par][h]
                S.add("act", lambda e, km=km, p_=p_, nch=nch: e.activation(out=km[:, 0:nch, :], in_=p_[:, 0:nch, :], func=AF.Copy), reads=[b_p], writes=[b_km])
                yield
                a_, b_a = pa.next()
                for ci in range(nch):
                    S.add("pe", lambda e, a_=a_, kt_=kt_, qt_=qt_, ci=ci: e.matmul(
                        a_[:, ci * CH:(ci + 1) * CH], lhsT=kt_[:, ci * CH:(ci + 1) * CH], rhs=qt_[:, ci * CH:(ci + 1) * CH], start=True, stop=True),
                        reads=[b_kt, b_qt], writes=[b_a])
                    if ci % 4 == 3:
                        yield
                am, b_am = attm[par][h]
                mk, b_mk = (k.maskF, k.b_maskF) if d_ == 0 else (k.maskB, k.b_maskB)
                mkv = mk[:].rearrange("s c t -> s (c t)")
                S.add("dve", lambda e, am=am, a_=a_, mkv=mkv, n=n: e.tensor_tensor(out=am[:, 0:n], in0=a_[:, 0:n], in1=mkv[:, 0:n], op=ALU.mult),
                      reads=[b_a, b_mk], writes=[b_am])
                yield

        def pump(gen, cnt):
            if gen is None:
                return None
            try:
                for _ in range(cnt):
                    next(gen)
            except StopIteration:
                return None
            return gen

        for d_ in range(2):
            for h in range(NH):
                S.add("pool", lambda e, h=h: e.memset(s32[h][0][:], 0.0), writes=[s32[h][1]])
                S.add("pool", lambda e, h=h: e.memset(sbf[h][0][:], 0.0), writes=[sbf[h][1]])
            if d_ == 0:
                order = list(k.groups)
            else:
                order = [k.groups[0]] + list(reversed(k.groups[1:]))
            g0 = prep_gen(d_, order[0][0], order[0][1], 0)
            while pump(g0, 1000) is not None:
                pass
            for gi, (t0, n) in enumerate(order):
                par = gi % 2
                nch = n // CH
                nxt = prep_gen(d_, order[gi + 1][0], order[gi + 1][1], 1 - par) if gi + 1 < len(order) else None
                for hp in range(NH // 2):
                    cis = range(nch) if d_ == 0 else range(nch - 1, -1, -1)
                    for ci in cis:
                        for j in range(2):
                            h = hp * 2 + j
                            po_, b_po = po[j]; sd_, b_sd = psd[j]
                            qt_, b_qt = qt[par][h]; km, b_km = ktm[par][h]; v_, b_v = vtm[par][h]; am, b_am = attm[par][h]
                            ee, b_ee = eend[par][h]; s_, b_s = s32[h]; r_, b_r = r32[h]; sb_, b_sb = sbf[h]
                            cs = slice(ci * CH, (ci + 1) * CH)
                            if d_ == 1:
                                S.add("dve", lambda e, r_=r_, s_=s_, ee=ee, ci=ci: e.tensor_scalar(out=r_[:], in0=s_[:], scalar1=ee[:, ci:ci + 1], scalar2=None, op0=ALU.mult),
                                      reads=[b_s, b_ee], writes=[b_r])
                                S.add("dve", lambda e, sb_=sb_, s_=s_, ee=ee, ci=ci: e.tensor_scalar(out=sb_[:], in0=s_[:], scalar1=ee[:, ci:ci + 1], scalar2=None, op0=ALU.mult),
                                      reads=[b_s, b_ee], writes=[b_sb])
                            S.add("pe", lambda e, po_=po_, v_=v_, am=am, ci=ci, cs=cs: e.matmul(po_[:, cs], lhsT=v_[:, ci, :], rhs=am[:, cs], start=True, stop=False),
                                  reads=[b_v, b_am], writes=[b_po])
                            S.add("pe", lambda e, po_=po_, sb_=sb_, qt_=qt_, cs=cs: e.matmul(po_[:, cs], lhsT=sb_[:], rhs=qt_[:, cs], start=False, stop=True),
                                  reads=[b_sb, b_qt], writes=[b_po])
                            S.add("pe", lambda e, sd_=sd_, km=km, v_=v_, ci=ci: e.matmul(sd_[:], lhsT=km[:, ci, :], rhs=v_[:, ci, :], start=True, stop=True),
                                  reads=[b_km, b_v], writes=[b_sd])
                            if d_ == 0:
                                S.add("dve", lambda e, r_=r_, s_=s_, sd_=sd_: e.tensor_tensor(out=r_[:], in0=sd_[:], in1=s_[:], op=ALU.add), reads=[b_sd, b_s], writes=[b_r])
                                S.add("dve", lambda e, sb_=sb_, r_=r_, ee=ee, ci=ci: e.tensor_scalar(out=sb_[:], in0=r_[:], scalar1=ee[:, ci:ci + 1], scalar2=None, op0=ALU.mult),
                                      reads=[b_r, b_ee], writes=[b_sb])
                                S.add("dve", lambda e, r_=r_, s_=s_, ee=ee, ci=ci: e.tensor_scalar(out=s_[:], in0=r_[:], scalar1=ee[:, ci:ci + 1], scalar2=None, op0=ALU.mult),
                                      reads=[b_r, b_ee], writes=[b_s])
                            else:
                                S.add("dve", lambda e, r_=r_, s_=s_, sd_=sd_: e.tensor_tensor(out=s_[:], in0=sd_[:], in1=r_[:], op=ALU.add), reads=[b_sd, b_r], writes=[b_s])
                            nxt = pump(nxt, 2)
                    for j in range(2):
                        h = hp * 2 + j
                        po_, b_po = po[j]
                        o_, b_o = osb.next()
                        S.add("act", lambda e, o_=o_, po_=po_, n=n: e.activation(out=o_[:, 0:n], in_=po_[:, 0:n], func=AF.Copy, scale=QSCALE), reads=[b_po], writes=[b_o])
                        dma(k, "act", k.od[d_][h * 128:(h + 1) * 128, t0:t0 + n], o_[:, 0:n], [b_o], [k.b_od[d_]])
                while nxt is not None:
                    nxt = pump(nxt, 1000)
        S.barrier()


def pass_lru(k, l):
    nc, S = k.nc, k.S
    TC, T, ROWS = k.TC, k.T, k.ROWS
    CO = 2
    LO = CO + TC + 4
    TOT = LO + T + 2
    GN = LRU_GN
    segs = [(CO, TC)] + [(LO + i * GN, GN) for i in range(T // GN)]
    with ExitStack() as es:
        big1, b_big1 = sbt(k, es, "l_big1", [128, TOT], F32)
        big2, b_big2 = sbt(k, es, "l_big2", [128, TOT], F32)
        xcb, b_xcb = sbt(k, es, "l_xcb", [128, TOT], BF16)
        hf, b_hf = sbt(k, es, "l_hf", [128, TOT], BF16)
        hb, b_hb = sbt(k, es, "l_hb", [128, TOT], BF16)
        gly, b_gly = sbt(k, es, "l_gly", [128, k.TA], BF16)
        tmpc, b_tmpc = sbt(k, es, "l_tmpc", [128, TC], F32)
        wts = [[sbt(k, es, f"l_w{d_}{g}", [128, 128], BF16) for g in range(2)] for d_ in range(2)]
        ii = Ring([sbt(k, es, f"l_ii{i}", [128, GN], F32) for i in range(2)])
        t1 = Ring([sbt(k, es, f"l_t1{i}", [128, GN], F32) for i in range(2)])
        ppr = Ring([pst(k, es, f"l_pr{i}", [128, GN]) for i in range(2)])
        ppi = Ring([pst(k, es, f"l_pi{i}", [128, GN]) for i in range(2)])
        for n in range(8):
            rows = slice(n * 128, (n + 1) * 128)
            dma(k, "sp", big1[:, 0:k.TA], k.lx[rows, :], [k.b_lx], [b_big1])
            dma(k, "sp", gly[:], k.gly[rows, :], [k.b_gly], [b_gly])
            for d_ in range(2):
                dma(k, "pool", wts[d_][0][0][:], k.lru_wa[l, d_, n], [k.b_in], [wts[d_][0][1]])
                dma(k, "pool", wts[d_][1][0][:], k.lru_wx[l, d_, n], [k.b_in], [wts[d_][1][1]])
            S.add("pool", lambda e: e.memset(big2[:], 0.0), writes=[b_big2])
            S.add("dve", lambda e: e.tensor_copy(out=big2[:, CO:CO + TC], in_=big1[:, 0:TC]), reads=[b_big1], writes=[b_big2])
            S.add("dve", lambda e: e.tensor_copy(out=big2[:, LO:LO + T].rearrange("p (c r) -> p c r", r=ROWS),
                                                  in_=big1[:, TC:TC + T].rearrange("p (r c) -> p c r", c=GW)), reads=[b_big1], writes=[b_big2])
            cws = [k.cw[:, (l * 4 + j) * 8 + n:(l * 4 + j) * 8 + n + 1] for j in range(4)]
            cwc = lambda j, cws=cws: cws[j]
            cbc = k.cb[:, l * 8 + n:l * 8 + n + 1]
            S.add("dve", lambda e, cwc=cwc, cbc=cbc: e.tensor_scalar(out=big1[:, 2:TOT - 1], in0=big2[:, 2:TOT - 1], scalar1=cwc(2), scalar2=cbc, op0=ALU.mult, op1=ALU.add),
                  reads=[b_big2, k.b_cw, k.b_cb], writes=[b_big1])
            for j, off in ((0, 0), (1, 1), (3, 3)):
                S.add("dve", lambda e, cwc=cwc, j=j, off=off: e.scalar_tensor_tensor(out=big1[:, 2:TOT - 1], in0=big2[:, off:TOT - 3 + off], scalar=cwc(j), in1=big1[:, 2:TOT - 1],
                                                                                  op0=ALU.mult, op1=ALU.add), reads=[b_big2, b_big1, k.b_cw], writes=[b_big1])
            S.add("pool", lambda e: e.tensor_copy(out=xcb[:, 2:TOT - 1], in_=big1[:, 2:TOT - 1]), reads=[b_big1], writes=[b_xcb])
            if n == 0 and l == 0:
                dump(k, "xc", big1[:, 0:TOT], b_big1, [128, TOT])
                dump(k, "xpad", big2[:, 0:TOT], b_big2, [128, TOT])
            for d_ in range(2):
                col = (l * 2 + d_) * 8 + n
                wa, b_wa = wts[d_][0]; wx, b_wx = wts[d_][1]
                for (s0, sn) in segs:
                    sl = slice(s0, s0 + sn)
                    pr, b_pr = ppr.next(); pi, b_pi = ppi.next()
                    i_, b_i = ii.next(); t_, b_t = t1.next()
                    for sub in range(0, sn, 512):
                        sw = min(512, sn - sub)
                        S.add("pe", lambda e, pr=pr, wa=wa, s0=s0, sub=sub, sw=sw: e.matmul(pr[:, sub:sub + sw], lhsT=wa[:], rhs=xcb[:, s0 + sub:s0 + sub + sw], start=True, stop=True),
                              reads=[b_wa, b_xcb], writes=[b_pr])
                        S.add("pe", lambda e, pi=pi, wx=wx, s0=s0, sub=sub, sw=sw: e.matmul(pi[:, sub:sub + sw], lhsT=wx[:], rhs=xcb[:, s0 + sub:s0 + sub + sw], start=True, stop=True),
                              reads=[b_wx, b_xcb], writes=[b_pi])
                    S.add("act", lambda e, pr=pr, sl=sl, sn=sn, col=col: e.activation(out=big1[:, sl], in_=pr[:, 0:sn], func=AF.Sigmoid, bias=k.ba[:, col:col + 1]), reads=[b_pr, k.b_ba], writes=[b_big1])
                    S.add("act", lambda e, i_=i_, pi=pi, sn=sn, col=col: e.activation(out=i_[:, 0:sn], in_=pi[:, 0:sn], func=AF.Sigmoid, bias=k.bx[:, col:col + 1]), reads=[b_pi, k.b_bx], writes=[b_i])
                    S.add("act", lambda e, sl=sl, col=col: e.activation(out=big1[:, sl], in_=big1[:, sl], func=AF.Exp, scale=k.asc[:, col:col + 1]), reads=[b_big1, k.b_asc], writes=[b_big1])
                    S.add("dve", lambda e, t_=t_, sl=sl, sn=sn: e.tensor_tensor(out=t_[:, 0:sn], in0=big1[:, sl], in1=big1[:, sl], op=ALU.mult), reads=[b_big1], writes=[b_t])
                    S.add("act", lambda e, t_=t_, sn=sn: e.activation(out=t_[:, 0:sn], in_=t_[:, 0:sn], func=AF.Sqrt, scale=-1.0, bias=1.0), reads=[b_t], writes=[b_t])
                    S.add("pool", lambda e, i_=i_, sl=sl, sn=sn: e.tensor_tensor(out=i_[:, 0:sn], in0=i_[:, 0:sn], in1=xcb[:, sl], op=ALU.mult), reads=[b_i, b_xcb], writes=[b_i])
                    S.add("dve", lambda e, t_=t_, i_=i_, sl=sl, sn=sn: e.tensor_tensor(out=big2[:, sl], in0=t_[:, 0:sn], in1=i_[:, 0:sn], op=ALU.mult), reads=[b_t, b_i], writes=[b_big2])
                ho, b_ho = (hf, b_hf) if d_ == 0 else (hb, b_hb)
                cseg = slice(CO, CO + TC); lseg = slice(LO, LO + T)
                if n == 0 and l == 0:
                    dump(k, f"a{d_}", big1[:, 0:TOT], b_big1, [128, TOT])
                    dump(k, f"u{d_}", big2[:, 0:TOT], b_big2, [128, TOT])
                if d_ == 0:
                    S.add("dve", lambda e: e.tensor_tensor_scan(out=tmpc[:], data0=big1[:, cseg], data1=big2[:, cseg], initial=0.0, op0=ALU.mult, op1=ALU.add),
                          reads=[b_big1, b_big2], writes=[b_tmpc])
                    S.add("dve", lambda e, ho=ho: e.tensor_tensor_scan(out=ho[:, lseg], data0=big1[:, lseg], data1=big2[:, lseg], initial=tmpc[:, TC - 1:TC], op0=ALU.mult, op1=ALU.add),
                          reads=[b_big1, b_big2, b_tmpc], writes=[b_ho])
                else:
                    S.add("dve", lambda e: e.tensor_tensor_scan(out=tmpc[:, ::-1], data0=big1[:, CO:CO + TC][:, ::-1], data1=big2[:, CO:CO + TC][:, ::-1], initial=0.0, op0=ALU.mult, op1=ALU.add),
                          reads=[b_big1, b_big2], writes=[b_tmpc])
                    S.add("dve", lambda e, ho=ho: e.tensor_tensor_scan(out=ho[:, LO:LO + T][:, ::-1], data0=big1[:, LO:LO + T][:, ::-1], data1=big2[:, LO:LO + T][:, ::-1],
                                                                       initial=tmpc[:, 0:1], op0=ALU.mult, op1=ALU.add),
                          reads=[b_big1, b_big2, b_tmpc], writes=[b_ho])
                S.add("pool", lambda e, ho=ho: e.tensor_copy(out=ho[:, cseg], in_=tmpc[:]), reads=[b_tmpc], writes=[b_ho])
            if n == 0 and l == 0:
                dump(k, "hf", hf[:, 0:TOT], b_hf, [128, TOT], BF16)
                dump(k, "hb", hb[:, 0:TOT], b_hb, [128, TOT], BF16)
            S.add("dve", lambda e: e.tensor_tensor(out=big1[:, CO:CO + TC], in0=hf[:, CO:CO + TC], in1=hb[:, CO:CO + TC], op=ALU.add), reads=[b_hf, b_hb], writes=[b_big1])
            S.add("dve", lambda e: e.tensor_tensor(out=big1[:, LO:LO + T], in0=hf[:, LO:LO + T], in1=hb[:, LO:LO + T], op=ALU.add), reads=[b_hf, b_hb], writes=[b_big1])
            S.add("dve", lambda e: e.tensor_tensor(out=xcb[:, 0:TC], in0=big1[:, CO:CO + TC], in1=gly[:, 0:TC], op=ALU.mult), reads=[b_big1, b_gly], writes=[b_xcb])
            S.add("dve", lambda e: e.tensor_tensor(out=xcb[:, TC:TC + T].rearrange("p (r c) -> p r c", c=GW),
                                                    in0=big1[:, LO:LO + T].rearrange("p (c r) -> p r c", r=ROWS),
                                                    in1=gly[:, TC:TC + T].rearrange("p (r c) -> p r c", c=GW), op=ALU.mult), reads=[b_big1, b_gly], writes=[b_xcb])
            dma(k, "act", k.gT[rows, :], xcb[:, 0:k.TA], [b_xcb], [k.b_gT])
        S.barrier()


def dump(k, name, ap, b, shape, dt=F32):
    if not k.debug:
        return
    d = k.nc.dram_tensor("dbg_" + name, list(shape), dt, kind="ExternalOutput").ap()
    dma(k, "sp", d, ap, [b], [k.S.buf("dbg_" + name)])


def bound_reg(k, eng, val):
    if not hasattr(k, "_breg"):
        k._breg = {}
    if val not in k._breg:
        k._breg[val] = eng.to_reg(val)
    return k._breg[val]


def emit_ln(k, z, b_z, xo, b_xo, lng, b_lng, lnb, b_lnb, st, b_st, mv, b_mv, rs, b_rs):
    S = k.S
    S.add("dve", lambda e: e.bn_stats(out=st[:, 0:6], in_=z[:, 0:512]), reads=[b_z], writes=[b_st])
    S.add("dve", lambda e: e.bn_stats(out=st[:, 6:12], in_=z[:, 512:1024]), reads=[b_z], writes=[b_st])
    S.add("dve", lambda e: e.bn_aggr(out=mv[:, 0:2], in_=st[:, 0:12]), reads=[b_st], writes=[b_mv])
    S.add("act", lambda e: e.activation(out=rs[:, 0:1], in_=mv[:, 1:2], func=AF.Sqrt, bias=k.epsln[:, 0:1]), reads=[b_mv, k.b_epsln], writes=[b_rs])
    S.add("dve", lambda e: e.reciprocal(out=rs[:, 1:2], in_=rs[:, 0:1]), reads=[b_rs], writes=[b_rs])
    S.add("dve", lambda e: e.tensor_scalar(out=xo[:], in0=z[:], scalar1=mv[:, 0:1], scalar2=rs[:, 1:2], op0=ALU.subtract, op1=ALU.mult), reads=[b_z, b_mv, b_rs], writes=[b_xo])
    S.add("pool", lambda e: e.tensor_tensor(out=xo[:], in0=xo[:], in1=lng[:], op=ALU.mult), reads=[b_xo, b_lng], writes=[b_xo])
    S.add("dve", lambda e: e.tensor_tensor(out=xo[:], in0=xo[:], in1=lnb[:], op=ALU.add), reads=[b_xo, b_lnb], writes=[b_xo])


def pass_merge(k, l):
    nc, S = k.nc, k.S
    last = (l == k.L - 1)
    with ExitStack() as es:
        bc = load_bc(k, es, l, ["g1", "sc2", "sh2"])
        lng, b_lng = load_row_bc(k, es, "r_lng", k.ln_g[l, 0:1, :])
        lnb, b_lnb = load_row_bc(k, es, "r_lnb", k.ln_b[l, 0:1, :])
        wa, b_wa = sbt(k, es, "r_wa", [128, 8, D], BF16)
        wb, b_wb = sbt(k, es, "r_wb", [128, 8, D], BF16)
        wo, b_wo = sbt(k, es, "r_wo", [128, 8, D], BF16)
        wr, b_wr = sbt(k, es, "r_wr", [128, 8, NE], F32)
        dma(k, "pool", wa[:], k.w_ba[l].rearrange("(k p) c -> p k c", p=128), [k.b_in], [b_wa])
        dma(k, "pool", wb[:], k.w_bb[l].rearrange("(k p) c -> p k c", p=128), [k.b_in], [b_wb])
        dma(k, "pool", wo[:], k.w_out[l].rearrange("(k p) c -> p k c", p=128), [k.b_in], [b_wo])
        dma(k, "sp", wr[:], k.w_router[l].rearrange("(k p) c -> p k c", p=128), [k.b_in], [b_wr])
        N = 128
        sets = []
        for i in range(2):
            o0, b_o0 = sbt(k, es, f"r_o0{i}", [128, 8, N], F32)
            o1, b_o1 = sbt(k, es, f"r_o1{i}", [128, 8, N], F32)
            sog, b_sog = sbt(k, es, f"r_sog{i}", [128, 8, N], BF16)
            gt, b_gt = sbt(k, es, f"r_gt{i}", [128, 8, N], BF16)
            sma, b_sma = sbt(k, es, f"r_sma{i}", [128, 8, N], BF16)
            smb, b_smb = sbt(k, es, f"r_smb{i}", [128, 8, N], BF16)
            oa, b_oa = sbt(k, es, f"r_oa{i}", [128, 8, N], BF16)
            mT, b_mT = sbt(k, es, f"r_mT{i}", [128, 8, N], BF16)
            sets.append((o0, b_o0, o1, b_o1, sog, b_sog, gt, b_gt, sma, b_sma, smb, b_smb, oa, b_oa, mT, b_mT))
        gset = Ring(sets)
        rsq = Ring([sbt(k, es, f"r_rsq{i}", [128, N], F32) for i in range(2)])
        rst = Ring([sbt(k, es, f"r_rst{i}", [128, N], F32) for i in range(2)])
        oa1 = Ring([sbt(k, es, f"r_oa1{i}", [128, N], F32) for i in range(2)])
        ma_ = Ring([sbt(k, es, f"r_ma{i}", [128, N], F32) for i in range(2)])
        mb_ = Ring([sbt(k, es, f"r_mb{i}", [128, N], F32) for i in range(2)])
        xt = Ring([sbt(k, es, f"r_x{i}", [128, D], F32) for i in range(2)])
        yv = Ring([sbt(k, es, f"r_yv{i}", [128, D], F32) for i in range(1)])
        zt = Ring([sbt(k, es, f"r_z{i}", [128, D], F32) for i in range(1)])
        x1t = Ring([sbt(k, es, f"r_x1{i}", [128, D], F32) for i in range(2)])
        h2f = Ring([sbt(k, es, f"r_h2f{i}", [128, D], F32) for i in range(2)])
        h2b = Ring([sbt(k, es, f"r_h2b{i}", [128, D], BF16) for i in range(2)])
        h2T = Ring([sbt(k, es, f"r_h2T{i}", [128, 8, 128], F32) for i in range(2)])
        stt = Ring([sbt(k, es, f"r_st{i}", [128, 12], F32) for i in range(2)])
        mvt = Ring([sbt(k, es, f"r_mv{i}", [128, 2], F32) for i in range(2)])
        rss = Ring([sbt(k, es, f"r_rs{i}", [128, 2], F32) for i in range(2)])
        sm = Ring([sbt(k, es, f"r_sm{i}", [128, 4], F32) for i in range(2)])
        ex = Ring([sbt(k, es, f"r_ex{i}", [128, NE], F32) for i in range(2)])
        aTs = Ring([sbt(k, es, f"r_aTs{i}", [NE, 128], F32) for i in range(2)])
        pm, b_pm = pst(k, es, "r_pm", [128, 512])
        pya, b_pya = pst(k, es, "r_pya", [128, 512])
        pyb, b_pyb = pst(k, es, "r_pyb", [128, 512])
        py = Ring([pst(k, es, f"r_py{i}", [128, 512]) for i in range(2)])
        ptr = [pst(k, es, f"r_ptr{i}", [128, 4, 128]) for i in range(2)]
        psm, b_psm = pst(k, es, "r_psm", [128, 512])

        def fmv(ap, t0, n):
            return ap.rearrange("(c p) t -> p c t", p=128)[:, :, t0:t0 + n]

        pendingB = [None]

        def group_body(gi, o0, b_o0, o1, b_o1, sog, b_sog, gt, b_gt, sma, b_sma, smb, b_smb, oa, b_oa, mT, b_mT):
            t0 = gi * N
            isctx = 1 if t0 < k.TC else 0
            if isctx and last:
                return None
            def f1():
                dma(k, "sp", o0[:], fmv(k.od[0], t0, N), [k.b_od[0]], [b_o0])
                dma(k, "sp", o1[:], fmv(k.od[1], t0, N), [k.b_od[1]], [b_o1])
                dma(k, "sp", sog[:], fmv(k.sog, t0, N), [k.b_sog], [b_sog])
                dma(k, "sp", gt[:], fmv(k.gT, t0, N), [k.b_gT], [b_gt])
                dma(k, "sp", sma[:], fmv(k.sma, t0, N), [k.b_sma], [b_sma])
                dma(k, "sp", smb[:], fmv(k.smb, t0, N), [k.b_smb], [b_smb])
                S.add("pool", lambda e: e.tensor_tensor(out=o0[:], in0=o0[:], in1=o1[:], op=ALU.add), reads=[b_o0, b_o1], writes=[b_o0])
                S.add("act", lambda e: e.activation(out=o1[:], in_=o0[:], func=AF.Square), reads=[b_o0], writes=[b_o1])
                for h in range(NH):
                    q_, b_q = rsq.next(); r_, b_r = rst.next(); a1, b_a1 = oa1.next()
                    S.add("pe", lambda e, h=h: e.matmul(pm[:, 0:N], lhsT=k.onesm[:], rhs=o1[:, h, :], start=True, stop=True), reads=[k.b_onesm, b_o1], writes=[b_pm])
                    S.add("act", lambda e, q_=q_: e.activation(out=q_[:], in_=pm[:, 0:N], func=AF.Sqrt, bias=k.epsrms[:, 0:1]), reads=[b_pm, k.b_epsrms], writes=[b_q])
                    S.add("dve", lambda e, q_=q_, r_=r_: e.reciprocal(out=r_[:], in_=q_[:]), reads=[b_q], writes=[b_r])
                    S.add("dve", lambda e, a1=a1, r_=r_, h=h: e.tensor_tensor(out=a1[:], in0=o0[:, h, :], in1=r_[:], op=ALU.mult), reads=[b_o0, b_r], writes=[b_a1])
                    S.add("dve", lambda e, a1=a1, h=h: e.scalar_tensor_tensor(out=oa[:, h, :], in0=a1[:], scalar=k.ng[:, l:l + 1], in1=sog[:, h, :], op0=ALU.mult, op1=ALU.mult),
                          reads=[b_a1, k.b_ng, b_sog], writes=[b_oa])
                    yield
                yield
            hold = {}

            def f2():
                for dc in range(8):
                    for c in range(8):
                        S.add("pe", lambda e, dc=dc, c=c: e.matmul(pya[:, 0:N], lhsT=wa[:, c, dc * 128:(dc + 1) * 128], rhs=oa[:, c, :], start=(c == 0), stop=(c == 7)),
                              reads=[b_wa, b_oa], writes=[b_pya])
                    for c in range(8):
                        S.add("pe", lambda e, dc=dc, c=c: e.matmul(pyb[:, 0:N], lhsT=wb[:, c, dc * 128:(dc + 1) * 128], rhs=gt[:, c, :], start=(c == 0), stop=(c == 7)),
                              reads=[b_wb, b_gt], writes=[b_pyb])
                    m1, b_m1 = ma_.next(); m2, b_m2 = mb_.next()
                    S.add("dve", lambda e, m1=m1, dc=dc: e.tensor_tensor(out=m1[:], in0=pya[:, 0:N], in1=sma[:, dc, :], op=ALU.mult), reads=[b_pya, b_sma], writes=[b_m1])
                    S.add("dve", lambda e, m2=m2, dc=dc: e.tensor_tensor(out=m2[:], in0=pyb[:, 0:N], in1=smb[:, dc, :], op=ALU.mult), reads=[b_pyb, b_smb], writes=[b_m2])
                    S.add("pool", lambda e, m1=m1, m2=m2, dc=dc: e.tensor_tensor(out=mT[:, dc, :], in0=m1[:], in1=m2[:], op=ALU.add), reads=[b_m1, b_m2], writes=[b_mT])
                    yield

            def f3():
                for tt in range(N // 128):
                    ti = (t0 + tt * 128) // 128
                    g1, b_g1 = bc[("g1", isctx)]; scp, b_scp = bc[("sc2", isctx)]; sh, b_sh = bc[("sh2", isctx)]
                    y_, b_y = yv.next(); x_, b_x = xt.next(); z_, b_z = zt.next(); x1_, b_x1 = x1t.next()
                    for half in range(2):
                        p_, b_p = py.next()
                        for c in range(8):
                            S.add("pe", lambda e, p_=p_, c=c, tt=tt, half=half: e.matmul(p_[:, :], lhsT=mT[:, c, tt * 128:(tt + 1) * 128], rhs=wo[:, c, half * 512:(half + 1) * 512],
                                                                                       start=(c == 0), stop=(c == 7)), reads=[b_mT, b_wo], writes=[b_p])
                        S.add("dve", lambda e, p_=p_, y_=y_, g1=g1, half=half: e.tensor_tensor(out=y_[:, half * 512:(half + 1) * 512], in0=p_[:, :], in1=g1[:, half * 512:(half + 1) * 512], op=ALU.mult),
                              reads=[b_p, b_g1], writes=[b_y])
                        yield
                    src, b_src = tile_src(k, l, ti)
                    dma(k, "sp", x_[:], src, [b_src], [b_x])
                    S.add("dve", lambda e, z_=z_, x_=x_, y_=y_: e.scalar_tensor_tensor(out=z_[:], in0=x_[:], scalar=ALPHA, in1=y_[:], op0=ALU.mult, op1=ALU.add), reads=[b_x, b_y], writes=[b_z])
                    st, b_st = stt.next(); mv, b_mv = mvt.next(); rs, b_rs = rss.next()
                    emit_ln(k, z_, b_z, x1_, b_x1, lng, b_lng, lnb, b_lnb, st, b_st, mv, b_mv, rs, b_rs)
                    dma(k, "act", k.x1[ti * 128:(ti + 1) * 128, :], x1_[:], [b_x1], [k.b_x1])
                    yield
                    hf_, b_hf = h2f.next(); hb_, b_hb = h2b.next()
                    S.add("pool", lambda e, hf_=hf_, x1_=x1_, scp=scp: e.tensor_tensor(out=hf_[:], in0=x1_[:], in1=scp[:], op=ALU.mult), reads=[b_x1, b_scp], writes=[b_hf])
                    S.add("dve", lambda e, hf_=hf_, sh=sh: e.tensor_tensor(out=hf_[:], in0=hf_[:], in1=sh[:], op=ALU.add), reads=[b_hf, b_sh], writes=[b_hf])
                    S.add("act", lambda e, hf_=hf_, hb_=hb_: e.activation(out=hb_[:], in_=hf_[:], func=AF.Copy), reads=[b_hf], writes=[b_hb])
                    dma(k, "act", k.h2[ti * 128:(ti + 1) * 128, :], hb_[:], [b_hb], [k.b_h2])
                    yield
                    def stageB(ti=ti, hf_=hf_, b_hf=b_hf):
                        hT_, b_hT = h2T.next()
                        for kk in range(8):
                            pt_, b_pt = ptr[kk // 4]
                            S.add("pe", lambda e, pt_=pt_, hf_=hf_, kk=kk: e.transpose(out=pt_[:, kk % 4, :], in_=hf_[:, kk * 128:(kk + 1) * 128], identity=k.identf[:]),
                                  reads=[b_hf, k.b_identf], writes=[b_pt])
                        for j in range(2):
                            pt_, b_pt = ptr[j]
                            S.add("act", lambda e, pt_=pt_, hT_=hT_, j=j: e.activation(out=hT_[:, j * 4:(j + 1) * 4, :], in_=pt_[:], func=AF.Copy), reads=[b_pt], writes=[b_hT])
                        yield
                        for kk in range(8):
                            S.add("pe", lambda e, hT_=hT_, kk=kk: e.matmul(psm[:, 0:NE], lhsT=hT_[:, kk, :], rhs=wr[:, kk, :], start=(kk == 0), stop=(kk == 7)), reads=[b_hT, b_wr], writes=[b_psm])
                        yield
                        s_, b_s = sm.next(); ex_, b_ex = ex.next()
                        S.add("dve", lambda e, s_=s_: e.tensor_reduce(out=s_[:, 0:1], in_=psm[:, 0:NE], axis=AX.X, op=ALU.max, negate=True), reads=[b_psm], writes=[b_s])
                        S.add("act", lambda e, s_=s_, ex_=ex_: e.activation(out=ex_[:], in_=psm[:, 0:NE], func=AF.Exp, bias=s_[:, 0:1], accum_out=s_[:, 1:2]), reads=[b_psm, b_s], writes=[b_ex, b_s])
                        S.add("dve", lambda e, s_=s_: e.reciprocal(out=s_[:, 2:3], in_=s_[:, 1:2]), reads=[b_s], writes=[b_s])
                        S.add("dve", lambda e, s_=s_, ex_=ex_, ti=ti: e.tensor_scalar(out=k.afft[:, ti, :], in0=ex_[:], scalar1=s_[:, 2:3], scalar2=None, op0=ALU.mult), reads=[b_ex, b_s], writes=[k.b_afft])
                        S.add("pe", lambda e, ti=ti: e.transpose(out=psm[0:NE, 128:256], in_=k.afft[:, ti, :], identity=k.identf[:]), reads=[k.b_afft, k.b_identf], writes=[b_psm])
                        at_, b_at = aTs.next()
                        S.add("act", lambda e, at_=at_: e.activation(out=at_[:], in_=psm[0:NE, 128:256], func=AF.Copy), reads=[b_psm], writes=[b_at])
                        dma(k, "act", k.affT[:, ti * 128:(ti + 1) * 128], at_[:], [b_at], [k.b_affT])
                        yield
                    hold['B'] = stageB

            return [f1, f2, f3, lambda: hold['B']()]
        tiles_st = []
        def run_round():
            gens = []
            for ent in list(tiles_st):
                gens.append(ent.pop(0)())
                if not ent:
                    tiles_st.remove(ent)
            while gens:
                for g in list(gens):
                    try:
                        next(g)
                    except StopIteration:
                        gens.remove(g)
        for gi in range(k.TA // N):
            st = group_body(gi, *gset.next())
            if st is not None:
                tiles_st.append(st)
            run_round()
        while tiles_st:
            run_round()
        S.barrier()


def pass_route(k, l):
    nc, S = k.nc, k.S
    last = (l == k.L - 1)
    TC, T, TA = k.TC, k.T, k.TA
    with ExitStack() as es:
        aff, b_aff = sbt(k, es, "t_aff", [NE, TA], F32)
        cmp_, b_cmp = sbt(k, es, "t_cmp", [NE, T], F32)
        sel, b_sel = sbt(k, es, "t_sel", [NE, T], F32)
        pos, b_pos = sbt(k, es, "t_pos", [NE, T], F32)
        idxf, b_idxf = sbt(k, es, "t_idxf", [NE, TA], F32)
        sv, b_sv = sbt(k, es, "t_sv", [NE, 8], F32)
        idxt, b_idxt = sbt(k, es, "t_idxt", [128, k.ntile, NE], F32)
        h2t = Ring([sbt(k, es, f"t_h2{i}", [128, D], BF16) for i in range(3)])
        pidx, b_pidx = pst(k, es, "t_pidx", [128, 32, NE])
        dma(k, "sp", aff[:], k.affT[:, :], [k.b_affT], [b_aff])
        sets = [(TC, T, k.cap_l, 0)]
        if not last:
            sets.append((0, TC, k.cap_c, k.cap_l))
        for (c0, n, cap, soff) in sets:
            av = aff[:, c0:c0 + n]
            lo, hi, mid, cnt, ge, dd, ee, ss = [sv[:, i:i + 1] for i in range(8)]
            S.add("dve", lambda e, lo=lo: e.memset(lo, 0.0), writes=[b_sv])
            S.add("dve", lambda e, hi=hi: e.memset(hi, 1.0), writes=[b_sv])
            S.add("dve", lambda e, mid=mid: e.memset(mid, 0.5), writes=[b_sv])
            for it in range(34):
                S.add("dve", lambda e, av=av, n=n, mid=mid, cnt=cnt: e.tensor_scalar(out=cmp_[:, 0:n], in0=av, scalar1=mid, scalar2=None, op0=ALU.is_ge, op1=ALU.add, accum_out=cnt),
                      reads=[b_aff, b_sv], writes=[b_cmp, b_sv])
                S.add("dve", lambda e, ge=ge, cnt=cnt, cap=cap: e.tensor_scalar(out=ge, in0=cnt, scalar1=float(cap) - 0.5, scalar2=None, op0=ALU.is_ge), reads=[b_sv], writes=[b_sv])
                S.add("dve", lambda e, dd=dd, mid=mid, lo=lo: e.tensor_sub(out=dd, in0=mid, in1=lo), reads=[b_sv], writes=[b_sv])
                S.add("dve", lambda e, ee=ee, mid=mid, hi=hi: e.tensor_sub(out=ee, in0=hi, in1=mid), reads=[b_sv], writes=[b_sv])
                S.add("dve", lambda e, dd=dd, ge=ge, lo=lo: e.scalar_tensor_tensor(out=lo, in0=dd, scalar=ge, in1=lo, op0=ALU.mult, op1=ALU.add), reads=[b_sv], writes=[b_sv])
                S.add("dve", lambda e, ee=ee, ge=ge, hi=hi, mid=mid: e.scalar_tensor_tensor(out=hi, in0=ee, scalar=ge, in1=mid, op0=ALU.mult, op1=ALU.add), reads=[b_sv], writes=[b_sv])
                S.add("dve", lambda e, ss=ss, lo=lo, hi=hi: e.tensor_add(out=ss, in0=lo, in1=hi), reads=[b_sv], writes=[b_sv])
                S.add("dve", lambda e, ss=ss, mid=mid: e.tensor_scalar(out=mid, in0=ss, scalar1=0.5, scalar2=None, op0=ALU.mult), reads=[b_sv], writes=[b_sv])
            S.add("dve", lambda e, av=av, n=n, lo=lo: e.tensor_scalar(out=sel[:, 0:n], in0=av, scalar1=lo, scalar2=None, op0=ALU.is_ge), reads=[b_aff, b_sv], writes=[b_sel])
            S.add("dve", lambda e, n=n: e.memset(cmp_[:, 0:n], 1.0), writes=[b_cmp])
            S.add("dve", lambda e, n=n: e.tensor_tensor_scan(out=pos[:, 0:n], data0=cmp_[:, 0:n], data1=sel[:, 0:n], initial=0.0, op0=ALU.mult, op1=ALU.add),
                  reads=[b_cmp, b_sel], writes=[b_pos])
            S.add("dve", lambda e, n=n, cap=cap: e.scalar_tensor_tensor(out=cmp_[:, 0:n], in0=pos[:, 0:n], scalar=float(cap) + 0.5, in1=sel[:, 0:n], op0=ALU.is_le, op1=ALU.mult),
                  reads=[b_pos, b_sel], writes=[b_cmp])
            S.add("dve", lambda e, n=n, soff=soff: e.tensor_scalar(out=pos[:, 0:n], in0=pos[:, 0:n], scalar1=BIGIDX + float(soff) - 1.0, scalar2=None, op0=ALU.add), reads=[b_pos], writes=[b_pos])
            S.add("dve", lambda e, n=n, c0=c0: e.scalar_tensor_tensor(out=idxf[:, c0:c0 + n], in0=cmp_[:, 0:n], scalar=-BIGIDX, in1=pos[:, 0:n], op0=ALU.mult, op1=ALU.add),
                  reads=[b_cmp, b_pos], writes=[b_idxf])
        tiles = [ti for ti in range(k.ntile) if not (last and ti * 128 < TC)]
        for b0 in range(0, len(tiles), 32):
            grp = tiles[b0:b0 + 32]
            for j, ti in enumerate(grp):
                S.add("pe", lambda e, j=j, ti=ti: e.transpose(out=pidx[:, j, :], in_=idxf[:, ti * 128:(ti + 1) * 128], identity=k.identf[0:NE, 0:NE]),
                      reads=[b_idxf, k.b_identf], writes=[b_pidx])
            S.add("act", lambda e, grp=grp: e.activation(out=idxt[:, grp[0]:grp[-1] + 1, :], in_=pidx[:, 0:len(grp), :], func=AF.Copy), reads=[b_pidx], writes=[b_idxt])
        S.add("dve", lambda e: e.scalar_tensor_tensor(out=k.gsel[:, tiles[0]:, :], in0=idxt[:, tiles[0]:, :], scalar=BIGIDX * 0.5, in1=k.afft[:, tiles[0]:, :], op0=ALU.is_lt, op1=ALU.mult),
              reads=[b_idxt, k.b_afft], writes=[k.b_gsel])
        S.add("dve", lambda e: e.tensor_tensor(out=idxt[:, tiles[0]:, :], in0=idxt[:, tiles[0]:, :], in1=k.eoff[:, tiles[0]:, :], op=ALU.add), reads=[b_idxt, k.b_eoff], writes=[b_idxt])
        S.add("dve", lambda e: e.tensor_copy(out=k.idxi[:, tiles[0]:, :], in_=idxt[:, tiles[0]:, :]), reads=[b_idxt], writes=[k.b_idxi])
        bound = NE * k.SLOTP - 1
        xs_flat = k.xs.rearrange("e s d -> (e s) d")
        for ti in tiles:
            h_, b_h = h2t.next()
            dma(k, "sp", h_[:], k.h2[ti * 128:(ti + 1) * 128, :], [k.b_h2], [b_h])
            for ex in range(NE):
                S.add("pool", lambda e, h_=h_, ti=ti, ex=ex: e.indirect_dma_start(
                    out=xs_flat, out_offset=bass.IndirectOffsetOnAxis(ap=k.idxi[:, ti, ex:ex + 1], axis=0), in_=h_[:], in_offset=None,
                    bounds_check=bound_reg(k, e, bound), oob_is_err=False), reads=[b_h, k.b_idxi], writes=[k.b_xs], dma=True, waw=False)
        S.barrier()


def pass_experts(k, l):
    nc, S = k.nc, k.S
    last = (l == k.L - 1)
    nslots = k.cap_l if last else k.SLOTS
    stiles = []
    s0 = 0
    while s0 < nslots:
        stiles.append((s0, min(128, nslots - s0)))
        s0 += 128
    sgroups = [stiles[i:i + 4] for i in range(0, len(stiles), 4)]
    with ExitStack() as es:
        wg = Ring([sbt(k, es, f"e_wg{i}", [128, 8, D], BF16) for i in range(2)])
        wu = Ring([sbt(k, es, f"e_wu{i}", [128, 8, D], BF16) for i in range(2)])
        wd = Ring([sbt(k, es, f"e_wd{i}", [128, 8, D], BF16) for i in range(2)])
        xt = Ring([sbt(k, es, f"e_x{i}", [128, D], BF16) for i in range(4)])
        xsT = Ring([sbt(k, es, f"e_xT{i}", [128, 8, 512], BF16) for i in range(2)])
        sgr = Ring([sbt(k, es, f"e_sg{i}", [128, 512], F32) for i in range(2)])
        hid = Ring([sbt(k, es, f"e_hid{i}", [128, 8, 512], BF16) for i in range(2)])
        yt = Ring([sbt(k, es, f"e_y{i}", [128, D], F32) for i in range(2)])
        pT = Ring([pst(k, es, f"e_pT{i}", [128, 8, 128], BF16) for i in range(2)])
        pg = Ring([pst(k, es, f"e_pg{i}", [128, 512]) for i in range(2)])
        pu = Ring([pst(k, es, f"e_pu{i}", [128, 512]) for i in range(2)])
        po = Ring([pst(k, es, f"e_po{i}", [128, 512]) for i in range(2)])
        for ex in range(NE):
            g_, b_g = wg.next(); u_, b_u = wu.next(); d_, b_d = wd.next()
            dma(k, "pool", g_[:], k.w_gate[l, ex].rearrange("(k p) c -> p k c", p=128), [k.b_in], [b_g])
            dma(k, "pool", u_[:], k.w_up[l, ex].rearrange("(k p) c -> p k c", p=128), [k.b_in], [b_u])
            dma(k, "pool", d_[:], k.w_down[l, ex].rearrange("(k p) c -> p k c", p=128), [k.b_in], [b_d])
            for grp in sgroups:
                xT_, b_xT = xsT.next()
                off = 0
                offs = []
                for (s0, r) in grp:
                    x_, b_x = xt.next(); p_, b_p = pT.next()
                    dma(k, "sp", x_[0:r, :], k.xs[ex, s0:s0 + r, :], [k.b_xs], [b_x])
                    for kk in range(8):
                        S.add("pe", lambda e, p_=p_, x_=x_, kk=kk, r=r: e.transpose(out=p_[:, kk, 0:r], in_=x_[0:r, kk * 128:(kk + 1) * 128], identity=k.identb[0:r, 0:r]),
                              reads=[b_x, k.b_identb], writes=[b_p])
                    S.add("act", lambda e, p_=p_, xT_=xT_, off=off, r=r: e.activation(out=xT_[:, :, off:off + r], in_=p_[:, :, 0:r], func=AF.Copy), reads=[b_p], writes=[b_xT])
                    offs.append(off)
                    off += r
                ncol = off
                h_, b_h = hid.next()
                for fc in range(8):
                    pg_, b_pg = pg.next(); pu_, b_pu = pu.next(); sg_, b_sg = sgr.next()
                    for kk in range(8):
                        S.add("pe", lambda e, pg_=pg_, g_=g_, xT_=xT_, kk=kk, fc=fc, ncol=ncol: e.matmul(pg_[:, 0:ncol], lhsT=g_[:, kk, fc * 128:(fc + 1) * 128], rhs=xT_[:, kk, 0:ncol],
                                                                                                     start=(kk == 0), stop=(kk == 7)), reads=[b_g, b_xT], writes=[b_pg])
                    for kk in range(8):
                        S.add("pe", lambda e, pu_=pu_, u_=u_, xT_=xT_, kk=kk, fc=fc, ncol=ncol: e.matmul(pu_[:, 0:ncol], lhsT=u_[:, kk, fc * 128:(fc + 1) * 128], rhs=xT_[:, kk, 0:ncol],
                                                                                                     start=(kk == 0), stop=(kk == 7)), reads=[b_u, b_xT], writes=[b_pu])
                    S.add("act", lambda e, sg_=sg_, pg_=pg_, ncol=ncol: e.activation(out=sg_[:, 0:ncol], in_=pg_[:, 0:ncol], func=AF.Silu), reads=[b_pg], writes=[b_sg])
                    S.add("dve", lambda e, h_=h_, sg_=sg_, pu_=pu_, fc=fc, ncol=ncol: e.tensor_tensor(out=h_[:, fc, 0:ncol], in0=pu_[:, 0:ncol], in1=sg_[:, 0:ncol], op=ALU.mult),
                          reads=[b_pu, b_sg], writes=[b_h])
                for (s0, r), o_ in zip(grp, offs):
                    y_, b_y = yt.next()
                    for half in range(2):
                        po_, b_po = po.next()
                        for fc in range(8):
                            S.add("pe", lambda e, po_=po_, h_=h_, d_=d_, fc=fc, o_=o_, r=r, half=half: e.matmul(po_[0:r, :], lhsT=h_[:, fc, o_:o_ + r], rhs=d_[:, fc, half * 512:(half + 1) * 512],
                                                                                                            start=(fc == 0), stop=(fc == 7)), reads=[b_h, b_d], writes=[b_po])
                        S.add("act", lambda e, po_=po_, y_=y_, r=r, half=half: e.activation(out=y_[0:r, half * 512:(half + 1) * 512], in_=po_[0:r, :], func=AF.Copy), reads=[b_po], writes=[b_y])
                    dma(k, "act", k.ys[ex, s0:s0 + r, :], y_[0:r, :], [b_y], [k.b_ys])
        S.barrier()


def pass_combine(k, l):
    nc, S = k.nc, k.S
    last = (l == k.L - 1)
    TC = k.TC
    with ExitStack() as es:
        bc = load_bc(k, es, l, ["g2"])
        lng, b_lng = load_row_bc(k, es, "c_lng", k.ln_g[l, 1:2, :])
        lnb, b_lnb = load_row_bc(k, es, "c_lnb", k.ln_b[l, 1:2, :])
        gb = [sbt(k, es, f"c_g{e_}", [128, D], F32) for e_ in range(NE)]
        acc = Ring([sbt(k, es, f"c_acc{i}", [128, D], F32) for i in range(2)])
        x1t = Ring([sbt(k, es, f"c_x1{i}", [128, D], F32) for i in range(2)])
        zt = Ring([sbt(k, es, f"c_z{i}", [128, D], F32) for i in range(2)])
        xo = Ring([sbt(k, es, f"c_xo{i}", [128, D], F32) for i in range(2)])
        stt = Ring([sbt(k, es, f"c_st{i}", [128, 12], F32) for i in range(2)])
        mvt = Ring([sbt(k, es, f"c_mv{i}", [128, 2], F32) for i in range(2)])
        rss = Ring([sbt(k, es, f"c_rs{i}", [128, 2], F32) for i in range(2)])
        for e_ in range(NE):
            S.add("dve", lambda e, e_=e_: e.memset(gb[e_][0][:], 0.0), writes=[gb[e_][1]])
        bound = NE * k.SLOTP - 1
        ys_flat = k.ys.rearrange("e s d -> (e s) d")
        tiles = [ti for ti in range(k.ntile) if not (last and ti * 128 < TC)]
        def tile_stages(ti):
            isctx = 1 if ti * 128 < TC else 0
            a_, b_a = acc.next(); x1_, b_x1 = x1t.next(); z_, b_z = zt.next(); o_, b_o = xo.next()

            def fa():
                dma(k, "sp", x1_[:], k.x1[ti * 128:(ti + 1) * 128, :], [k.b_x1], [b_x1])
                for e_ in range(NE):
                    g_, b_g = gb[e_]
                    S.add("pool", lambda e, g_=g_, e_=e_: e.indirect_dma_start(
                        out=g_[:], out_offset=None, in_=ys_flat, in_offset=bass.IndirectOffsetOnAxis(ap=k.idxi[:, ti, e_:e_ + 1], axis=0),
                        bounds_check=bound_reg(k, e, bound), oob_is_err=False), reads=[k.b_ys, k.b_idxi], writes=[b_g], dma=True, waw=False)
                    if e_ == 0:
                        S.add("dve", lambda e, g_=g_, e_=e_: e.tensor_scalar(out=a_[:], in0=g_[:], scalar1=k.gsel[:, ti, e_:e_ + 1], scalar2=None, op0=ALU.mult),
                              reads=[b_g, k.b_gsel], writes=[b_a])
                    else:
                        S.add("dve", lambda e, g_=g_, e_=e_: e.scalar_tensor_tensor(out=a_[:], in0=g_[:], scalar=k.gsel[:, ti, e_:e_ + 1], in1=a_[:], op0=ALU.mult, op1=ALU.add),
                              reads=[b_g, k.b_gsel, b_a], writes=[b_a])
                    yield

            def fb():
                g2, b_g2 = bc[("g2", isctx)]
                S.add("pool", lambda e: e.tensor_tensor(out=a_[:], in0=a_[:], in1=g2[:], op=ALU.mult), reads=[b_a, b_g2], writes=[b_a])
                yield
                S.add("dve", lambda e: e.scalar_tensor_tensor(out=z_[:], in0=x1_[:], scalar=ALPHA, in1=a_[:], op0=ALU.mult, op1=ALU.add), reads=[b_x1, b_a], writes=[b_z])
                yield
                st, b_st = stt.next(); mv, b_mv = mvt.next(); rs, b_rs = rss.next()
                S.add("dve", lambda e: e.bn_stats(out=st[:, 0:6], in_=z_[:, 0:512]), reads=[b_z], writes=[b_st])
                yield
                S.add("dve", lambda e: e.bn_stats(out=st[:, 6:12], in_=z_[:, 512:1024]), reads=[b_z], writes=[b_st])
                S.add("dve", lambda e: e.bn_aggr(out=mv[:, 0:2], in_=st[:, 0:12]), reads=[b_st], writes=[b_mv])
                S.add("act", lambda e: e.activation(out=rs[:, 0:1], in_=mv[:, 1:2], func=AF.Sqrt, bias=k.epsln[:, 0:1]), reads=[b_mv, k.b_epsln], writes=[b_rs])
                yield
                S.add("dve", lambda e: e.reciprocal(out=rs[:, 1:2], in_=rs[:, 0:1]), reads=[b_rs], writes=[b_rs])
                S.add("dve", lambda e: e.tensor_scalar(out=o_[:], in0=z_[:], scalar1=mv[:, 0:1], scalar2=rs[:, 1:2], op0=ALU.subtract, op1=ALU.mult), reads=[b_z, b_mv, b_rs], writes=[b_o])
                yield
                S.add("pool", lambda e: e.tensor_tensor(out=o_[:], in0=o_[:], in1=lng[:], op=ALU.mult), reads=[b_o, b_lng], writes=[b_o])
                yield
                S.add("dve", lambda e: e.tensor_tensor(out=o_[:], in0=o_[:], in1=lnb[:], op=ALU.add), reads=[b_o, b_lnb], writes=[b_o])
                if last:
                    r0 = ti * 128 - TC
                    dma(k, "act", k.out[r0:r0 + 128, :], o_[:], [b_o], [k.b_out])
                else:
                    dma(k, "act", k.xn[ti * 128:(ti + 1) * 128, :], o_[:], [b_o], [k.b_xn])
                yield
            return [fa, fb]

        tiles_st = []

        def run_round():
            gens = []
            for ent in list(tiles_st):
                gens.append(ent.pop(0)())
                if not ent:
                    tiles_st.remove(ent)
            while gens:
                for g in list(gens):
                    try:
                        next(g)
                    except StopIteration:
                        gens.remove(g)
        for ti in tiles:
            tiles_st.append(tile_stages(ti))
            run_round()
        while tiles_st:
            run_round()
        S.barrier()


_NC_CACHE = {}
PER_SAMPLE = ("x", "c", "ctx")


def make_in_map(inputs, b):
    m = {}
    for name, v in inputs.items():
        v = np.asarray(v)
        if name in PER_SAMPLE:
            m[name] = np.ascontiguousarray(v[b])
        else:
            m[name] = np.ascontiguousarray(v)
    return m


def kernel(**inputs):
    x = np.asarray(inputs["x"])
    B, T, _ = x.shape
    TC = np.asarray(inputs["ctx"]).shape[1]
    key = (T, TC)
    if key not in _NC_CACHE:
        _NC_CACHE[key] = build_program(T, TC)
    nc = _NC_CACHE[key]
    ncores = 8
    maps = [make_in_map(inputs, i % B) for i in range(ncores)]
    res = run_bass_kernel_spmd(nc, maps, core_ids=list(range(ncores)))
    out = np.stack([np.asarray(res.results[b]["out"]) for b in range(B)], axis=0)
    return out.astype(np.float32)
```
